# Optimizing a Trainium2 kernel written in Bass

```python
import math
import jax
import jax.numpy as jnp
from jax import lax
import numpy as np


D_MODEL = 2048
BATCH = 4
SEQ = 2048
DEPTH = 4
DEC_BATCH = 2
DEC_SEQ = 4096
PAST_LEN = 128

N_MIXERS = 2
N_HGRN = (DEPTH + 1) // 2
N_ATTN = DEPTH // 2
HG_HEADS = 16
HG_DK = D_MODEL // HG_HEADS
HG_DV = D_MODEL // HG_HEADS
HG_CHUNK = 64
DA_HEADS = 8
DA_DH = D_MODEL // DA_HEADS // 2
DA_DV = 2 * DA_DH
ROPE_DIM = DA_DH // 4
ROPE_THETA = 500000.0
Q_BLOCK = 128
D_FF = 5632
CONV_W = 3
EPS = 1e-6
F_FLOOR = 1e-30

kernel_name = 'hybrid_hgrn2_diffattn_convffn_encoder'


def rmsnorm(x, g):
    xf = x.astype(jnp.float32)
    y = xf * lax.rsqrt(jnp.mean(xf * xf, axis=-1, keepdims=True) + EPS)
    return (y * g.astype(jnp.float32)).astype(x.dtype)


def gla_chunk_scan(q, k, v, logf):
    B, L, H, DK = q.shape
    DV = v.shape[-1]
    n = L // HG_CHUNK

    def to_chunks(t):
        return t.reshape(B, n, HG_CHUNK, H, t.shape[-1]).transpose(1, 0, 3, 2, 4)

    qc, kc, vc, gc = to_chunks(q), to_chunks(k), to_chunks(v), to_chunks(logf)
    mask = jnp.tril(jnp.ones((HG_CHUNK, HG_CHUNK), dtype=bool))[:, :, None]

    def step(S, inp):
        qi, ki, vi, gi = inp
        b = jnp.cumsum(gi, axis=-2)
        diff = b[..., :, None, :] - b[..., None, :, :]
        decay = jnp.where(mask, jnp.exp(jnp.where(mask, diff, 0.0)), 0.0)
        scores = jnp.einsum('bhtd,bhsd,bhtsd->bhts', qi, ki, decay)
        o = jnp.einsum('bhts,bhsv->bhtv', scores, vi) + jnp.einsum('bhtd,bhdv->bhtv', qi * jnp.exp(b), S)
        b_last = b[..., -1:, :]
        S = jnp.exp(b_last[..., 0, :])[..., None] * S + jnp.einsum('bhsd,bhsv->bhdv', ki * jnp.exp(b_last - b), vi)
        return S, o

    S0 = jnp.zeros((B, H, DK, DV), jnp.float32)
    _, o = lax.scan(step, S0, (qc, kc, vc, gc))
    return o.transpose(1, 0, 3, 2, 4).reshape(B, L, H, DV)


def hgrn2_mixer(h, w_in, w_out, norm_g, lb):
    B, L, _ = h.shape
    proj = (h @ w_in).astype(jnp.float32).reshape(B, L, 5, HG_HEADS, HG_DK)
    q = jax.nn.silu(proj[:, :, 0])
    z_fw = proj[:, :, 1]
    z_bw = proj[:, :, 2]
    v = proj[:, :, 3]
    gate = proj[:, :, 4]
    lbh = lb.astype(jnp.float32).reshape(HG_HEADS, HG_DK)

    def gates(z):
        f = lbh + (1.0 - lbh) * jax.nn.sigmoid(z)
        logf = jnp.log(jnp.maximum(f, F_FLOOR))
        k = (1.0 - lbh) * jax.nn.sigmoid(-z)
        return logf, k

    logf_fw, k_fw = gates(z_fw)
    logf_bw, k_bw = gates(z_bw)
    o_fw = gla_chunk_scan(q, k_fw, v, logf_fw)
    flip = lambda t: jnp.flip(t, axis=1)
    o_bw = flip(gla_chunk_scan(flip(q), flip(k_bw), flip(v), flip(logf_bw)))
    o = rmsnorm(o_fw + o_bw, norm_g) * jax.nn.silu(gate)
    return o.reshape(B, L, D_MODEL).astype(h.dtype) @ w_out


def rope_tables(L):
    inv = 1.0 / (ROPE_THETA ** (jnp.arange(0, ROPE_DIM, 2, dtype=jnp.float32) / ROPE_DIM))
    ang = jnp.arange(L, dtype=jnp.float32)[:, None] * inv[None, :]
    return jnp.cos(ang), jnp.sin(ang)


def apply_partial_rope(x, cos, sin):
    xf = x.astype(jnp.float32)
    half = ROPE_DIM // 2
    x1 = xf[..., :half]
    x2 = xf[..., half:ROPE_DIM]
    c = cos[None, :, None, :]
    s = sin[None, :, None, :]
    rot = jnp.concatenate([x1 * c - x2 * s, x2 * c + x1 * s], axis=-1)
    return jnp.concatenate([rot, xf[..., ROPE_DIM:]], axis=-1).astype(x.dtype)


def diff_attn_mixer(h, w_qkv, w_out, lam_params, subln_g, lambda_init, cos, sin):
    B, L, _ = h.shape
    qkv = h @ w_qkv
    q = qkv[..., :D_MODEL].reshape(B, L, 2 * DA_HEADS, DA_DH)
    k = qkv[..., D_MODEL:2 * D_MODEL].reshape(B, L, 2 * DA_HEADS, DA_DH)
    v = qkv[..., 2 * D_MODEL:].reshape(B, L, DA_HEADS, DA_DV)
    q = apply_partial_rope(q, cos, sin) * (DA_DH ** -0.5)
    k = apply_partial_rope(k, cos, sin)
    lp = lam_params.astype(jnp.float32)
    lam = jnp.exp(jnp.sum(lp[0] * lp[1])) - jnp.exp(jnp.sum(lp[2] * lp[3])) + lambda_init
    nb = L // Q_BLOCK
    qb = q.reshape(B, nb, Q_BLOCK, 2 * DA_HEADS, DA_DH).transpose(1, 0, 2, 3, 4)

    def block(qi):
        s = jnp.einsum('bqhd,bkhd->bhqk', qi, k, preferred_element_type=jnp.float32)
        p = jax.nn.softmax(s, axis=-1).reshape(B, DA_HEADS, 2, Q_BLOCK, L)
        a = p[:, :, 0] - lam * p[:, :, 1]
        return jnp.einsum('bhqk,bkhv->bqhv', a.astype(v.dtype), v)

    o = lax.map(block, qb)
    o = o.transpose(1, 0, 2, 3, 4).reshape(B, L, DA_HEADS, DA_DV)
    o = rmsnorm(o, subln_g) * (1.0 - lambda_init)
    return o.reshape(B, L, D_MODEL).astype(h.dtype) @ w_out


def conv_ffn(h, w_up, conv_w, conv_b, w_down):
    u = h @ w_up
    up = jnp.pad(u, ((0, 0), (1, 1), (0, 0)))
    u = up[:, :-2] * conv_w[0] + up[:, 1:-1] * conv_w[1] + up[:, 2:] * conv_w[2] + conv_b
    gate = u[..., :D_FF]
    val = u[..., D_FF:]
    return (jax.nn.gelu(gate, approximate=True) * val) @ w_down


def trunk(x, pre_mix_g, post_mix_g, pre_ffn_g, post_ffn_g, hg_w_in, hg_w_out, hg_norm_g, hg_lower_bounds,
          da_w_qkv, da_w_out, da_lambda, da_subln_g, ffn_w_up, ffn_conv_w, ffn_conv_b, ffn_w_down):
    L = x.shape[1]
    cos, sin = rope_tables(L)
    sm = jax.nn.softmax(hg_lower_bounds.astype(jnp.float32), axis=0)
    lbs = jnp.cumsum(sm, axis=0) - sm[0:1]
    for i in range(DEPTH):
        j = i // N_MIXERS
        h = rmsnorm(x, pre_mix_g[i])
        if i % N_MIXERS == 0:
            m = hgrn2_mixer(h, hg_w_in[j], hg_w_out[j], hg_norm_g[j], lbs[i])
        else:
            lambda_init = 0.8 - 0.6 * math.exp(-0.3 * i)
            m = diff_attn_mixer(h, da_w_qkv[j], da_w_out[j], da_lambda[j], da_subln_g[j], lambda_init, cos, sin)
        x = x + rmsnorm(m, post_mix_g[i])
        h = rmsnorm(x, pre_ffn_g[i])
        x = x + rmsnorm(conv_ffn(h, ffn_w_up[i], ffn_conv_w[i], ffn_conv_b[i], ffn_w_down[i]), post_ffn_g[i])
    return x


def setup_inputs(seed: int = 0) -> dict:
    key = jax.random.key(seed)
    ks = jax.random.split(key, 20)
    nrm = lambda k, shape, s: jax.random.normal(k, shape, jnp.float32) * s
    gain = lambda k, shape: 1.0 + 0.01 * jax.random.normal(k, shape, jnp.float32)
    return {
        'x_prompt': nrm(ks[0], (BATCH, SEQ, D_MODEL), 1.0),
        'x_sample': nrm(ks[1], (DEC_BATCH, DEC_SEQ, D_MODEL), 1.0),
        'pre_mix_g': gain(ks[2], (DEPTH, D_MODEL)),
        'post_mix_g': gain(ks[3], (DEPTH, D_MODEL)),
        'pre_ffn_g': gain(ks[4], (DEPTH, D_MODEL)),
        'post_ffn_g': gain(ks[5], (DEPTH, D_MODEL)),
        'hg_w_in': nrm(ks[6], (N_HGRN, D_MODEL, 5 * D_MODEL), D_MODEL ** -0.5),
        'hg_w_out': nrm(ks[7], (N_HGRN, D_MODEL, D_MODEL), D_MODEL ** -0.5),
        'hg_norm_g': gain(ks[8], (N_HGRN, HG_DV)),
        'hg_lower_bounds': nrm(ks[9], (DEPTH, D_MODEL), 0.1),
        'da_w_qkv': nrm(ks[10], (N_ATTN, D_MODEL, 3 * D_MODEL), D_MODEL ** -0.5),
        'da_w_out': nrm(ks[11], (N_ATTN, D_MODEL, D_MODEL), D_MODEL ** -0.5),
        'da_lambda': nrm(ks[12], (N_ATTN, 4, DA_DH), 0.1),
        'da_subln_g': gain(ks[13], (N_ATTN, DA_DV)),
        'ffn_w_up': nrm(ks[14], (DEPTH, D_MODEL, 2 * D_FF), D_MODEL ** -0.5),
        'ffn_conv_w': nrm(ks[15], (DEPTH, CONV_W, 2 * D_FF), CONV_W ** -0.5),
        'ffn_conv_b': nrm(ks[16], (DEPTH, 2 * D_FF), 0.01),
        'ffn_w_down': nrm(ks[17], (DEPTH, D_FF, D_MODEL), D_FF ** -0.5),
    }


def reference(x_prompt, x_sample, pre_mix_g, post_mix_g, pre_ffn_g, post_ffn_g, hg_w_in, hg_w_out, hg_norm_g,
              hg_lower_bounds, da_w_qkv, da_w_out, da_lambda, da_subln_g, ffn_w_up, ffn_conv_w, ffn_conv_b,
              ffn_w_down):
    y_prompt = trunk(x_prompt, pre_mix_g, post_mix_g, pre_ffn_g, post_ffn_g, hg_w_in, hg_w_out, hg_norm_g,
                     hg_lower_bounds, da_w_qkv, da_w_out, da_lambda, da_subln_g, ffn_w_up, ffn_conv_w, ffn_conv_b,
                     ffn_w_down)
    y_sample = trunk(x_sample, pre_mix_g, post_mix_g, pre_ffn_g, post_ffn_g, hg_w_in, hg_w_out, hg_norm_g,
                     hg_lower_bounds, da_w_qkv, da_w_out, da_lambda, da_subln_g, ffn_w_up, ffn_conv_w, ffn_conv_b,
                     ffn_w_down)
    return (y_prompt, y_sample)
```

```python
import contextlib
import math
import numpy as np
import ml_dtypes
import concourse.bass as bass
import concourse.mybir as mybir
from concourse.bass_utils import run_bass_kernel_spmd

F32 = mybir.dt.float32
BF16 = mybir.dt.bfloat16
U8 = mybir.dt.uint8
AF = mybir.ActivationFunctionType
ALU = mybir.AluOpType
BF = ml_dtypes.bfloat16

D = 2048
KC = 16
DFF = 5632
FC = 44
NH = 16
CH = 64
EPS = 1e-6
DEPTH = 4
ROPE_THETA = 500000.0

ENGS = ("pe", "act", "dve", "pool", "sp")
ENGOBJ = {"pe": "tensor", "act": "scalar", "dve": "vector", "pool": "gpsimd", "sp": "sync"}


class Buf:
    def __init__(self, name):
        self.name = name
        self.last_write = None
        self.reads = []
        self.sem = None
        self.dma_count = 0
        self.sem_id = None
        self.base = 0


class Instr:
    __slots__ = ("eng", "fn", "waits", "marked", "idx", "is_dma", "dbuf", "dval")

    def __init__(self, eng, fn, is_dma=False):
        self.eng = eng; self.fn = fn; self.waits = []; self.marked = False
        self.idx = None; self.is_dma = is_dma; self.dbuf = None; self.dval = None


class Sched:
    def __init__(self, nc, same_engine_sync=True):
        self.nc = nc
        self.streams = {e: [] for e in ENGS}
        self.waited = {e: {} for e in ENGS}
        self.same = same_engine_sync
        self.bufs = []
        self.free_sems = []
        self.nsem = 0
        self.phase_bufs = None

    def buf(self, name):
        b = Buf(name); self.bufs.append(b)
        if self.phase_bufs is not None:
            self.phase_bufs.append(b)
        return b

    def release(self, bufs):
        for b in bufs:
            if b.sem_id is not None:
                self.free_sems.append((b.sem_kind, b.sem_id, b.base + b.dma_count))

    def _dep(self, ins, dep):
        if dep is None or dep is ins:
            return
        if dep.is_dma:
            key = ("dma", id(dep.dbuf)); val = dep.dval
        else:
            if dep.eng == ins.eng and (dep.eng == "pe" or not self.same):
                return
            key = ("eng", dep.eng); val = dep.idx
        w = self.waited[ins.eng]
        if w.get(key, -1) >= val:
            return
        w[key] = val
        dep.marked = True
        ins.waits.append(dep)

    def op(self, eng, fn, reads=(), writes=(), dma_dst=None):
        is_dma = dma_dst is not None
        ins = Instr(eng, fn, is_dma)
        st = self.streams[eng]
        ins.idx = len(st)
        if is_dma:
            if dma_dst.sem_id is None:
                kind = "sw" if eng == "pool" else "hw"
                dma_dst.sem_kind = kind
                cand = [x for x in self.free_sems if x[0] == kind]
                if cand:
                    x = cand[-1]
                    self.free_sems.remove(x)
                    dma_dst.sem_id, dma_dst.base = x[1], x[2]
                else:
                    dma_dst.sem_id, dma_dst.base = self.nsem, 0
                    self.nsem += 1
            dma_dst.dma_count += 1
            ins.dbuf = dma_dst; ins.dval = dma_dst.dma_count
            dma_dst.last_dma = ins
        for b in reads:
            self._dep(ins, b.last_write)
        for b in writes:
            self._dep(ins, b.last_write)
            for r in b.reads:
                self._dep(ins, r)
        for b in writes:
            b.last_write = ins; b.reads = []
        for b in reads:
            b.reads.append(ins)
        st.append(ins)
        return ins

    def barrier(self):
        last = {}
        for e in ENGS:
            for ins in reversed(self.streams[e]):
                if (not ins.is_dma) and ins.fn is not None:
                    last[e] = ins
                    break
        dmas = [b.last_dma for b in self.bufs if getattr(b, "last_dma", None) is not None]
        news = []
        for e in ENGS:
            ins = Instr(e, None)
            ins.idx = len(self.streams[e])
            for e2 in ENGS:
                if e2 in last:
                    self._dep(ins, last[e2])
            for d in dmas:
                self._dep(ins, d)
            news.append(ins)
        for ins in news:
            self.streams[ins.eng].append(ins)
        for b in self.bufs:
            b.last_dma = None

    def emit(self, es, final_bufs=()):
        nc = self.nc
        esem = {e: es.enter_context(nc.semaphore("s_" + e)) for e in ENGS}
        dsems = [es.enter_context(nc.semaphore("d%d" % i)) for i in range(self.nsem)]
        for b in self.bufs:
            if b.sem_id is not None:
                b.sem = dsems[b.sem_id]
        cnt = {}
        for e in ENGS:
            c = 0
            for ins in self.streams[e]:
                if (not ins.is_dma) and ins.marked:
                    c += 1
                    cnt[id(ins)] = c
        block = es.enter_context(nc.Block())

        def run(e, eng):
            for ins in self.streams[e]:
                for d in ins.waits:
                    if d.is_dma:
                        eng.wait_ge(d.dbuf.sem, 16 * (d.dbuf.base + d.dval))
                    else:
                        eng.wait_ge(esem[d.eng], cnt[id(d)])
                if ins.fn is None:
                    continue
                h = ins.fn(eng)
                if ins.is_dma:
                    h.then_inc(ins.dbuf.sem, 16)
                elif ins.marked:
                    h.then_inc(esem[e], 1)
            if e == "sp":
                for b in final_bufs:
                    if b.dma_count:
                        eng.wait_ge(b.sem, 16 * (b.base + b.dma_count))

        for e in ENGS:
            getattr(block, ENGOBJ[e])(lambda eng, e=e: run(e, eng))


class TL:
    def __init__(self, t, b):
        self.t = t; self.b = b


SP_CONVW = 0
SP_CONVB = SP_CONVW + 4 * 3 * 88
SP_LB = SP_CONVB + 4 * 88
SP_HGN = SP_LB + 64
SP_SUBLN = SP_HGN + 2
SP_LAM = SP_SUBLN + 4
SP_FLAG = SP_LAM + 8
NSP = SP_FLAG + 1


def needed(layers, parts):
    need = set()
    if "ffn" in parts:
        need.add("ffn")
    if "mix" in parts:
        for l in layers:
            need.add("hg" if l % 2 == 0 else "da")
    return need


WNAMES = {"hg": ("hg_w_in", "hg_w_out"), "da": ("da_w_qk", "da_w_v", "da_w_out"), "ffn": ("ffn_w_up", "ffn_w_down")}


def bcast(ap, n):
    return bass.AP(ap.tensor, ap.offset, [list(x) for x in ap.ap] + [[0, n]])


class Prog:
    def __init__(self, T, layers=(0, 1, 2, 3), parts=("mix", "ffn")):
        self.T = T
        self.MT = T // 2
        self.NTT = T // 128
        self.layers = layers
        self.parts = parts
        self.nc = bass.Bass("TRN2", target_bir_lowering=False)
        self.S = Sched(self.nc)
        self.es = contextlib.ExitStack()
        self.uid = 0

    def sb(self, name, shape, dt):
        self.uid += 1
        t = self.es.enter_context(self.nc.sbuf_tensor("%s_%d" % (name, self.uid), list(shape), dt))
        return TL(t, self.S.buf(name))

    def dram(self, name, shape, dt, kind="Internal"):
        t = self.nc.dram_tensor(name, list(shape), dt, kind=kind).ap()
        return TL(t, self.S.buf(name))

    def op(self, eng, fn, reads=(), writes=(), dma_dst=None):
        return self.S.op(eng, fn, [r.b for r in reads], [w.b for w in writes], None if dma_dst is None else dma_dst.b)

    def dma(self, eng, out_ap, in_ap, dst, reads=(), extra_writes=(), **kw):
        return self.op(eng, lambda e: e.dma_start(out=out_ap, in_=in_ap, **kw), reads=reads,
                       writes=[dst] + list(extra_writes), dma_dst=dst)

    @contextlib.contextmanager
    def phase(self):
        old = self.es
        self.S.phase_bufs = []
        with contextlib.ExitStack() as es:
            self.es = es
            yield
        self.es = old
        self.S.barrier()
        pb = self.S.phase_bufs
        self.S.phase_bufs = None
        self.S.release(pb)

    def build(self):
        nc, T, MT = self.nc, self.T, self.MT
        P = self
        self.x_in = P.dram("x_in", [T, D], F32, "ExternalInput")
        self.y = P.dram("y", [T, D], F32, "ExternalOutput")
        self.xs = P.dram("xs", [T, D], F32)
        self.gains = P.dram("gains", [16, 128, D], F32, "ExternalInput")
        self.spd = P.dram("spd", [128, NSP], F32, "ExternalInput")
        self.cmat = P.dram("cmat", [128, 4, 128], BF16, "ExternalInput")
        self.cs = P.dram("cs", [128, 2, T], F32, "ExternalInput")
        self.maskb = P.dram("maskb", [128, 2, T // 128], F32, "ExternalInput")
        need = needed(self.layers, self.parts)
        if "hg" in need:
            self.hg_w_in = P.dram("hg_w_in", [2, 80, 128, KC, 128], F32, "ExternalInput")
            self.hg_w_out = P.dram("hg_w_out", [2, 128, KC, D], F32, "ExternalInput")
            for nm in ("qs_d", "gs_d", "vT_d", "kf_d", "kb_d", "qcf_d", "qcb_d"):
                setattr(self, nm, P.dram(nm, [D, T], BF16))
            for nm in ("lf_d", "lb_d", "o_d"):
                setattr(self, nm, P.dram(nm, [D, T], F32))
            self.S_d = P.dram("S_d", [NH, 2, 2, 128, 128], F32)
        if "da" in need:
            self.da_w_qk = P.dram("da_w_qk", [2, 32, 128, KC, 128], F32, "ExternalInput")
            self.da_w_v = P.dram("da_w_v", [2, 128, KC, D], F32, "ExternalInput")
            self.da_w_out = P.dram("da_w_out", [2, 128, KC, D], F32, "ExternalInput")
            self.qT_d = P.dram("qT_d", [D, T], BF16)
            self.kT_d = P.dram("kT_d", [D, T], BF16)
            self.v_d = P.dram("v_d", [T, D], BF16)
        if "ffn" in need:
            self.ffn_w_up = P.dram("ffn_w_up", [4, 88, 128, KC, 128], F32, "ExternalInput")
            self.ffn_w_down = P.dram("ffn_w_down", [4, 128, FC, D], F32, "ExternalInput")
        self.hT_d = P.dram("hT_d", [D, T + 2], BF16)
        self.oT_d = P.dram("oT_d", [D, T], BF16)
        self.ones = P.sb("ones", [128, 128], BF16)
        P.op("dve", lambda e: e.memset(self.ones.t[:], 1.0), writes=[self.ones])

        self.spt = P.sb("spt", [128, NSP], F32)
        self.cm = P.sb("cm", [128, 4, 128], BF16)
        self.ps = []
        for i in range(8):
            t = self.es.enter_context(nc.psum_tensor("ps%d" % i, [128, 512], F32))
            self.ps.append(TL(t, self.S.buf("ps%d" % i)))
        self.epsc = P.sb("epsc", [128, 1], F32)
        P.op("dve", lambda e: e.memset(self.epsc.t[:], EPS), writes=[self.epsc])
        P.dma("sp", self.spt.t[:], self.spd.t, self.spt)
        P.dma("sp", self.cm.t[:], self.cmat.t, self.cm)
        z = P.sb("zero", [128, KC, 2], BF16)
        P.op("dve", lambda e: e.memset(z.t[:], 0.0), writes=[z])
        hv = self.hT_d.t.rearrange("(k p) t -> p k t", p=128)
        P.dma("sp", hv[:, :, 0:1], z.t[:, :, 0:1], self.hT_d, reads=[z], allow_slow_non_contiguous=True)
        P.dma("sp", hv[:, :, T + 1:T + 2], z.t[:, :, 1:2], self.hT_d, reads=[z], allow_slow_non_contiguous=True)

        chain = []
        subl = [(l, p) for l in self.layers for p in ("mix", "ffn") if p in self.parts]
        n = len(subl)
        bufs = [self.xs, self.y]
        cur = self.x_in
        for i, (l, p) in enumerate(subl):
            dst = bufs[(n - i) % 2]
            chain.append((l, p, cur, dst))
            cur = dst
        for (l, p, src, dst) in chain:
            if p == "ffn":
                self.ffn(l, src, dst)
            elif l % 2 == 0:
                self.hgrn(l, src, dst)
            else:
                self.attn(l, src, dst)
        self.S.emit(self.es, final_bufs=[self.y.b])
        self.es.close()
        return nc

    def spcol(self, c, n=1):
        return self.spt.t[:, c:c + n]

    def load_gain(self, idx, tl):
        self.dma("sp", tl.t[:], self.gains.t[idx], tl)

    def rstd(self, ss, rs, n, parts=128):
        self.op("act", lambda e: e.activation(out=rs.t[:parts], in_=ss.t[:parts], func=AF.Sqrt, scale=1.0 / n, bias=self.epsc.t[:parts, 0:1]),
                reads=[ss, self.epsc], writes=[rs])
        self.op("dve", lambda e: e.reciprocal(out=rs.t[:parts], in_=rs.t[:parts]), reads=[rs], writes=[rs])

    def norm_phase(self, src, gidx):
        P, nc, T = self, self.nc, self.T
        with P.phase():
            g = P.sb("ng", [128, D], F32)
            P.load_gain(gidx, g)
            xt = [P.sb("nx%d" % i, [128, D], F32) for i in range(2)]
            junk = P.sb("njunk", [128, D], BF16)
            hb = [P.sb("nhb%d" % i, [128, D], BF16) for i in range(2)]
            ss = [P.sb("nss%d" % i, [128, 1], F32) for i in range(2)]
            rs = [P.sb("nrs%d" % i, [128, 1], F32) for i in range(2)]
            hTt = [P.sb("nhT%d" % i, [128, KC, 128], BF16) for i in range(2)]
            hv = self.hT_d.t.rearrange("(k p) t -> p k t", p=128)
            P.dma("sp", xt[0].t[:], src.t[0:128, :], xt[0], reads=[src])
            for i in range(self.NTT):
                s = i % 2
                if i + 1 < self.NTT:
                    P.dma("sp", xt[1 - s].t[:], src.t[(i + 1) * 128:(i + 2) * 128, :], xt[1 - s], reads=[src])
                P.op("act", lambda e, s=s: e.activation(out=junk.t[:], in_=xt[s].t[:], func=AF.Square, accum_out=ss[s].t[:]),
                     reads=[xt[s]], writes=[junk, ss[s]])
                P.rstd(ss[s], rs[s], D)
                P.op("dve", lambda e, s=s: e.scalar_tensor_tensor(out=hb[s].t[:], in0=xt[s].t[:], scalar=rs[s].t[:, 0:1], in1=g.t[:],
                                                                  op0=ALU.mult, op1=ALU.mult), reads=[xt[s], rs[s], g], writes=[hb[s]])
                for half in range(2):
                    pb = self.ps[half]
                    pv = pb.t.bitcast(BF16)
                    for kk in range(8):
                        k = half * 8 + kk
                        P.op("pe", lambda e, s=s, k=k, kk=kk, pv=pv: e.transpose(pv[:, kk * 128:(kk + 1) * 128], hb[s].t[:, k * 128:(k + 1) * 128], self.cm.t[:, 0, :]),
                             reads=[hb[s], self.cm], writes=[pb])
                    P.op("act" if half == 0 else "dve",
                         (lambda e, s=s, half=half, pv=pv: e.activation(out=hTt[s].t[:, half * 8:(half + 1) * 8, :], in_=pv[:, :].rearrange("p (k t) -> p k t", k=8), func=AF.Copy))
                         if half == 0 else
                         (lambda e, s=s, half=half, pv=pv: e.tensor_copy(out=hTt[s].t[:, half * 8:(half + 1) * 8, :], in_=pv[:, :].rearrange("p (k t) -> p k t", k=8))),
                         reads=[pb], writes=[hTt[s]])
                P.dma("sp", hv[:, :, 1 + i * 128:1 + (i + 1) * 128], hTt[s].t[:], self.hT_d, reads=[hTt[s]])

    def post_tile(self, m_ap, m_tl, gpost, src, dst, tt, tiles):
        P = self
        xt, junk, ss, rs, tmp = tiles["xt"], tiles["junk"], tiles["ss"], tiles["rs"], tiles["tmp"]
        P.dma("sp", xt.t[:], src.t[tt * 128:(tt + 1) * 128, :], xt, reads=[src])
        P.op("act", lambda e: e.activation(out=junk.t[:], in_=m_ap, func=AF.Square, accum_out=ss.t[:]),
             reads=[m_tl], writes=[junk, ss])
        P.rstd(ss, rs, D)
        P.op("dve", lambda e: e.scalar_tensor_tensor(out=tmp.t[:], in0=m_ap, scalar=rs.t[:, 0:1], in1=gpost.t[:],
                                                     op0=ALU.mult, op1=ALU.mult), reads=[m_tl, rs, gpost], writes=[tmp])
        P.op("dve", lambda e: e.tensor_tensor(out=tmp.t[:], in0=tmp.t[:], in1=xt.t[:], op=ALU.add), reads=[tmp, xt], writes=[tmp])
        P.dma("sp", dst.t[tt * 128:(tt + 1) * 128, :], tmp.t[:], dst, reads=[tmp])

    def post_tiles_alloc(self, i):
        P = self
        return {"xt": P.sb("pxt%d" % i, [128, D], F32), "junk": P.sb("pjunk%d" % i, [128, D], BF16),
                "ss": P.sb("pss%d" % i, [128, 1], F32), "rs": P.sb("prs%d" % i, [128, 1], F32),
                "tmp": P.sb("ptmp%d" % i, [128, D], F32)}

    def ffn(self, l, src, dst):
        P, nc, T, MT = self, self.nc, self.T, self.MT
        P.norm_phase(src, 2 * 4 + l)
        TF = 512
        NG = T // TF
        with P.phase():
            gpost = P.sb("fg", [128, D], F32)
            P.load_gain(3 * 4 + l, gpost)
            hTw = [P.sb("fhT%d" % i, [128, KC, TF + 2], BF16) for i in range(2)]
            NWS = 3
            wup = [P.sb("fwu%d" % i, [128, KC, 128], BF16) for i in range(NWS)]
            U = [P.sb("fU%d" % i, [128, TF + 2], F32) for i in range(2)]
            acc = [P.sb("facc%d" % i, [128, TF], F32) for i in range(2)]
            gl = P.sb("fgl", [128, TF], F32)
            gT = P.sb("fgT", [128, FC, TF], BF16)
            NWD = 3
            wd = [P.sb("fwd%d" % i, [128, 4, 512], BF16) for i in range(NWD)]
            msb = [P.sb("fm%d" % i, [128, D], F32) for i in range(4)]
            pt = [P.post_tiles_alloc(i) for i in range(2)]
            hv = self.hT_d.t.rearrange("(k p) t -> p k t", p=128)
            wi = 0
            wdi = 0
            flag = P.spcol(SP_FLAG)
            for tg in range(NG):
                hw = hTw[tg % 2]
                P.dma("sp", hw.t[:], hv[:, :, tg * TF:tg * TF + TF + 2], hw, reads=[self.hT_d])
                if tg * TF == MT:
                    P.op("dve", lambda e, hw=hw: e.tensor_scalar(out=hw.t[:, :, 0:1], in0=hw.t[:, :, 0:1], scalar1=flag, scalar2=None, op0=ALU.mult),
                         reads=[hw, self.spt], writes=[hw])
                if (tg + 1) * TF == MT:
                    P.op("dve", lambda e, hw=hw: e.tensor_scalar(out=hw.t[:, :, TF + 1:TF + 2], in0=hw.t[:, :, TF + 1:TF + 2], scalar1=flag, scalar2=None, op0=ALU.mult),
                         reads=[hw, self.spt], writes=[hw])
                for j in range(FC):
                    for part in range(2):
                        mch = part * FC + j
                        w = wup[wi % NWS]; wi += 1
                        P.dma("pool", w.t[:], self.ffn_w_up.t[l, mch], w)
                        pm = self.ps[(2 * j + part) % 2]
                        ph = self.ps[2]
                        hcol = ((2 * j + part) % 2) * 2
                        for k in range(KC):
                            P.op("pe", lambda e, w=w, hw=hw, k=k, pm=pm: e.matmul(pm.t[:, :], lhsT=w.t[:, k, :], rhs=hw.t[:, k, 1:TF + 1], start=(k == 0), stop=(k == KC - 1)),
                                 reads=[w, hw], writes=[pm])
                        for k in range(KC):
                            P.op("pe", lambda e, w=w, hw=hw, k=k, ph=ph, hcol=hcol: e.matmul(ph.t[:, hcol:hcol + 2], lhsT=w.t[:, k, :], rhs=hw.t[:, k, 0:TF + 2:TF + 1], start=(k == 0), stop=(k == KC - 1)),
                                 reads=[w, hw], writes=[ph])
                        u = U[part]
                        P.op("act", lambda e, u=u, pm=pm: e.activation(out=u.t[:, 1:TF + 1], in_=pm.t[:, :], func=AF.Copy), reads=[pm], writes=[u])
                        P.op("dve", lambda e, u=u, ph=ph, hcol=hcol: e.tensor_copy(out=u.t[:, 0:TF + 2:TF + 1], in_=ph.t[:, hcol:hcol + 2]), reads=[ph], writes=[u])
                        a = acc[part]
                        cw = lambda jj, mch=mch: P.spcol(SP_CONVW + (l * 3 + jj) * 88 + mch)
                        P.op("dve", lambda e, u=u, a=a, cw=cw: e.tensor_scalar(out=a.t[:], in0=u.t[:, 0:TF], scalar1=cw(0), scalar2=None, op0=ALU.mult),
                             reads=[u, self.spt], writes=[a])
                        P.op("dve", lambda e, u=u, a=a, cw=cw: e.scalar_tensor_tensor(out=a.t[:], in0=u.t[:, 1:TF + 1], scalar=cw(1), in1=a.t[:], op0=ALU.mult, op1=ALU.add),
                             reads=[u, a, self.spt], writes=[a])
                        P.op("dve", lambda e, u=u, a=a, cw=cw: e.scalar_tensor_tensor(out=a.t[:], in0=u.t[:, 2:TF + 2], scalar=cw(2), in1=a.t[:], op0=ALU.mult, op1=ALU.add),
                             reads=[u, a, self.spt], writes=[a])
                    bg = P.spcol(SP_CONVB + l * 88 + j)
                    bv = P.spcol(SP_CONVB + l * 88 + FC + j)
                    P.gelu(gl, acc[0], bg)
                    P.op("dve", lambda e, j=j, bv=bv: e.scalar_tensor_tensor(out=gT.t[:, j, :], in0=acc[1].t[:], scalar=bv, in1=gl.t[:], op0=ALU.add, op1=ALU.mult),
                         reads=[acc[1], gl, self.spt], writes=[gT])
                for cb in range(4):
                    for kq in range(FC // 4):
                        w = wd[wdi % NWD]; wdi += 1
                        P.dma("pool", w.t[:], self.ffn_w_down.t[l, :, kq * 4:(kq + 1) * 4, cb * 512:(cb + 1) * 512], w)
                        for kk in range(4):
                            k = kq * 4 + kk
                            for sub in range(4):
                                pd = self.ps[4 + sub]
                                P.op("pe", lambda e, w=w, kk=kk, k=k, sub=sub, pd=pd: e.matmul(pd.t[:, :], lhsT=gT.t[:, k, sub * 128:(sub + 1) * 128], rhs=w.t[:, kk, :], start=(k == 0), stop=(k == FC - 1)),
                                     reads=[w, gT], writes=[pd])
                    for sub in range(4):
                        pd = self.ps[4 + sub]
                        P.op("act", lambda e, sub=sub, cb=cb, pd=pd: e.activation(out=msb[sub].t[:, cb * 512:(cb + 1) * 512], in_=pd.t[:, :], func=AF.Copy),
                             reads=[pd], writes=[msb[sub]])
                for sub in range(4):
                    P.post_tile(msb[sub].t[:], msb[sub], gpost, src, dst, tg * 4 + sub, pt[sub % 2])

    def gelu(self, out, a, bias):
        P = self
        if getattr(self, "native_gelu", True):
            P.op("act", lambda e: e.activation(out=out.t[:], in_=a.t[:], func=AF.Gelu_apprx_tanh, bias=bias), reads=[a, self.spt], writes=[out])
        else:
            P.op("act", lambda e: e.activation(out=a.t[:], in_=a.t[:], func=AF.Identity, bias=bias), reads=[a, self.spt], writes=[a])
            P.op("dve", lambda e: e.tensor_tensor(out=out.t[:], in0=a.t[:], in1=a.t[:], op=ALU.mult), reads=[a], writes=[out])
            P.op("dve", lambda e: e.tensor_scalar(out=out.t[:], in0=out.t[:], scalar1=0.044715, scalar2=1.0, op0=ALU.mult, op1=ALU.add), reads=[out], writes=[out])
            P.op("dve", lambda e: e.tensor_tensor(out=out.t[:], in0=out.t[:], in1=a.t[:], op=ALU.mult), reads=[out, a], writes=[out])
            P.op("act", lambda e: e.activation(out=out.t[:], in_=out.t[:], func=AF.Sigmoid, scale=1.5957691216057308), reads=[out], writes=[out])
            P.op("dve", lambda e: e.tensor_tensor(out=out.t[:], in0=out.t[:], in1=a.t[:], op=ALU.mult), reads=[out, a], writes=[out])

    def out_proj(self, w_dram, gidx, src, dst):
        P, T = self, self.T
        with P.phase():
            gpost = P.sb("og", [128, D], F32)
            P.load_gain(gidx, gpost)
            W = P.sb("oW", [128, KC, D], BF16)
            for q in range(KC):
                P.dma("pool", W.t[:, q, :], w_dram[:, q, :], W)
            og = [P.sb("oo%d" % i, [128, KC, 512], BF16) for i in range(2)]
            msb = [P.sb("om%d" % i, [128, D], F32) for i in range(2)]
            pt = [P.post_tiles_alloc(i) for i in range(2)]
            ov = self.oT_d.t.rearrange("(k p) t -> p k t", p=128)
            for tg in range(T // 512):
                o = og[tg % 2]
                P.dma("sp", o.t[:], ov[:, :, tg * 512:(tg + 1) * 512], o, reads=[self.oT_d])
                for sub in range(4):
                    m = msb[sub % 2]
                    for cb in range(4):
                        pd = self.ps[4 + cb]
                        for k in range(KC):
                            P.op("pe", lambda e, o=o, k=k, sub=sub, cb=cb, pd=pd: e.matmul(pd.t[:, :], lhsT=o.t[:, k, sub * 128:(sub + 1) * 128], rhs=W.t[:, k, cb * 512:(cb + 1) * 512], start=(k == 0), stop=(k == KC - 1)),
                                 reads=[o, W], writes=[pd])
                        P.op("act", lambda e, m=m, cb=cb, pd=pd: e.activation(out=m.t[:, cb * 512:(cb + 1) * 512], in_=pd.t[:, :], func=AF.Copy), reads=[pd], writes=[m])
                    P.post_tile(m.t[:], m, gpost, src, dst, tg * 4 + sub, pt[sub % 2])


    def hgrn(self, l, src, dst):
        P, nc, T, MT = self, self.nc, self.T, self.MT
        j = l // 2
        NCH = MT // CH
        P.norm_phase(src, 0 * 4 + l)
        TP = min(T, 2048)
        hv = self.hT_d.t.rearrange("(k p) t -> p k t", p=128)
        if not hasattr(self, "lbv"):
            self.lbv = P.sb("lbv", [128, 16], F32); self.omlb = P.sb("omlb", [128, 16], F32)
            self.lbE = P.sb("lbE", [128, 64], F32); self.lbn = P.sb("lbn", [128, 16], F32); self.lbd = P.sb("lbd", [128, 16], F32)
        lbv, omlb, E4, lbn, lbd = self.lbv, self.omlb, self.lbE, self.lbn, self.lbd
        if l == 0:
            P.op("dve", lambda e: e.memset(lbv.t[:], 0.0), writes=[lbv])
        else:
            P.op("act", lambda e: e.activation(out=E4.t[:, :], in_=self.spt.t[:, SP_LB:SP_LB + 64], func=AF.Exp), reads=[self.spt], writes=[E4])
            P.op("dve", lambda e: e.tensor_tensor(out=lbd.t[:, :], in0=E4.t[:, 0:16], in1=E4.t[:, 16:32], op=ALU.add), reads=[E4], writes=[lbd])
            P.op("dve", lambda e: e.tensor_tensor(out=lbd.t[:, :], in0=lbd.t[:, :], in1=E4.t[:, 32:48], op=ALU.add), reads=[E4, lbd], writes=[lbd])
            P.op("dve", lambda e: e.tensor_tensor(out=lbd.t[:, :], in0=lbd.t[:, :], in1=E4.t[:, 48:64], op=ALU.add), reads=[E4, lbd], writes=[lbd])
            P.op("dve", lambda e: e.tensor_copy(out=lbn.t[:, :], in_=E4.t[:, 16:32]), reads=[E4], writes=[lbn])
            for i in range(2, l + 1):
                P.op("dve", lambda e, i=i: e.tensor_tensor(out=lbn.t[:, :], in0=lbn.t[:, :], in1=E4.t[:, i * 16:(i + 1) * 16], op=ALU.add), reads=[E4, lbn], writes=[lbn])
            P.op("dve", lambda e: e.reciprocal(out=lbd.t[:, :], in_=lbd.t[:, :]), reads=[lbd], writes=[lbd])
            P.op("dve", lambda e: e.tensor_tensor(out=lbv.t[:, :], in0=lbn.t[:, :], in1=lbd.t[:, :], op=ALU.mult), reads=[lbn, lbd], writes=[lbv])
        P.op("dve", lambda e: e.tensor_scalar(out=omlb.t[:, :], in0=lbv.t[:, :], scalar1=-1.0, scalar2=1.0, op0=ALU.mult, op1=ALU.add), reads=[lbv], writes=[omlb])
        with P.phase():
            hT = P.sb("hhT", [128, KC, TP], BF16)
            wq = [P.sb("hwq%d" % i, [128, KC, 128], BF16) for i in range(3)]
            sg = [P.sb("hsg%d" % i, [128, 512], F32) for i in range(2)]
            wv_ = [P.sb("hwv%d" % i, [128, 512], F32) for i in range(2)]
            lft = [P.sb("hlf%d" % i, [128, 512], F32) for i in range(2)]
            kk = [P.sb("hkk%d" % i, [128, 512], BF16) for i in range(2)]
            ob = [P.sb("hob%d" % i, [128, 512], BF16) for i in range(3)]
            wi = 0; it = 0; oi = 0; gi = 0
            for p in range(T // TP):
                P.dma("sp", hT.t[:], hv[:, :, 1 + p * TP:1 + (p + 1) * TP], hT, reads=[self.hT_d])
                for h in range(NH):
                    for qty in range(5):
                        mch = qty * 16 + h
                        w = wq[wi % 3]; wi += 1
                        P.dma("pool", w.t[:], self.hg_w_in.t[j, mch], w)
                        for tb in range(TP // 512):
                            pm = self.ps[it % 4]; it += 1
                            for k in range(KC):
                                P.op("pe", lambda e, w=w, k=k, tb=tb, pm=pm: e.matmul(pm.t[:, :], lhsT=w.t[:, k, :], rhs=hT.t[:, k, tb * 512:(tb + 1) * 512], start=(k == 0), stop=(k == KC - 1)),
                                     reads=[w, hT], writes=[pm])
                            r0 = h * 128; c0 = p * TP + tb * 512
                            if qty in (0, 4, 3):
                                o = ob[oi % 3]; oi += 1
                                if qty == 3:
                                    P.op("dve", lambda e, o=o, pm=pm: e.tensor_copy(out=o.t[:, :], in_=pm.t[:, :]), reads=[pm], writes=[o])
                                else:
                                    P.op("act", lambda e, o=o, pm=pm: e.activation(out=o.t[:, :], in_=pm.t[:, :], func=AF.Silu), reads=[pm], writes=[o])
                                dd = {0: self.qs_d, 4: self.gs_d, 3: self.vT_d}[qty]
                                P.dma("sp", dd.t[r0:r0 + 128, c0:c0 + 512], o.t[:, :], dd, reads=[o])
                            else:
                                g_ = gi % 2; gi += 1
                                P.op("act", lambda e, g_=g_, pm=pm: e.activation(out=sg[g_].t[:, :], in_=pm.t[:, :], func=AF.Sigmoid), reads=[pm], writes=[sg[g_]])
                                P.op("dve", lambda e, g_=g_, h=h: e.tensor_scalar(out=wv_[g_].t[:, :], in0=sg[g_].t[:, :], scalar1=1e-30, scalar2=omlb.t[:, h:h + 1], op0=ALU.max, op1=ALU.mult),
                                     reads=[sg[g_], omlb], writes=[wv_[g_]])
                                P.op("act", lambda e, g_=g_, h=h: e.activation(out=lft[g_].t[:, :], in_=wv_[g_].t[:, :], func=AF.Ln, bias=lbv.t[:, h:h + 1]), reads=[wv_[g_], lbv], writes=[lft[g_]])
                                P.op("dve", lambda e, g_=g_, h=h: e.tensor_scalar(out=kk[g_].t[:, :], in0=wv_[g_].t[:, :], scalar1=-1.0, scalar2=omlb.t[:, h:h + 1], op0=ALU.mult, op1=ALU.add),
                                     reads=[wv_[g_], omlb], writes=[kk[g_]])
                                ld = self.lf_d if qty == 1 else self.lb_d
                                kd = self.kf_d if qty == 1 else self.kb_d
                                P.dma("sp", ld.t[r0:r0 + 128, c0:c0 + 512], lft[g_].t[:, :], ld, reads=[lft[g_]])
                                P.dma("sp", kd.t[r0:r0 + 128, c0:c0 + 512], kk[g_].t[:, :], kd, reads=[kk[g_]])
        with P.phase():
            ones32 = P.sb("sone", [128, MT], F32)
            P.op("dve", lambda e: e.memset(ones32.t[:], 1.0), writes=[ones32])
            q_t = P.sb("sq", [128, MT], BF16); v_t = P.sb("sv", [128, MT], BF16)
            k_t = [P.sb("sk%d" % d, [128, MT], BF16) for d in range(2)]
            lf_t = [P.sb("slf%d" % d, [128, MT], F32) for d in range(2)]
            Bp = [P.sb("sBp%d" % d, [128, MT + 1], F32) for d in range(2)]
            D1 = P.sb("sD1", [128, MT], F32); E = P.sb("sE", [128, MT], F32)
            qt = [P.sb("sqt%d" % d, [128, MT], BF16) for d in range(2)]
            kt = [P.sb("skt%d" % d, [128, MT], BF16) for d in range(2)]
            kh = [P.sb("skh%d" % d, [128, MT], BF16) for d in range(2)]
            qc = P.sb("sqc", [128, MT], BF16)
            r = [P.sb("sr%d" % d, [128, NCH], F32) for d in range(2)]
            rn = [P.sb("srn%d" % d, [128, NCH], F32) for d in range(2)]
            g = [P.sb("sg%d" % d, [128, NCH], F32) for d in range(2)]
            nb = P.sb("snb", [128, 1], F32)
            v_tok = P.sb("svt", [64, NCH, 128], BF16)
            kh_tok = [P.sb("skt_%d" % d, [64, NCH, 128], BF16) for d in range(2)]
            M32 = [[P.sb("sM%d%d" % (d, i), [128, 128], F32) for i in range(2)] for d in range(2)]
            Mb = [[P.sb("sMb%d%d" % (d, i), [128, 128], BF16) for i in range(2)] for d in range(2)]
            ATm = [P.sb("sAT%d" % i, [64, 64], BF16) for i in range(4)]
            o32 = P.sb("so32", [128, MT], F32)
            for d in range(2):
                P.op("dve", lambda e, d=d: e.memset(Bp[d].t[:, 0:1], 0.0), writes=[Bp[d]])
            pA = self.ps[0]; pU = self.ps[1]
            ai = 0; ui = 0; ti = 0
            for h in range(NH):
                r0 = h * 128
                for seg in range(2):
                    c0 = seg * MT
                    P.dma("sp", q_t.t[:], self.qs_d.t[r0:r0 + 128, c0:c0 + MT], q_t, reads=[self.qs_d])
                    P.dma("sp", v_t.t[:], self.vT_d.t[r0:r0 + 128, c0:c0 + MT], v_t, reads=[self.vT_d])
                    for d in range(2):
                        kd = self.kf_d if d == 0 else self.kb_d
                        ld = self.lf_d if d == 0 else self.lb_d
                        P.dma("sp", k_t[d].t[:], kd.t[r0:r0 + 128, c0:c0 + MT], k_t[d], reads=[kd])
                        P.dma("sp", lf_t[d].t[:], ld.t[r0:r0 + 128, c0:c0 + MT], lf_t[d], reads=[ld])
                    def tok_T(srct, dstt):
                        nonlocal ti
                        for c in range(NCH):
                            pb = self.ps[2 + (ti // 8) % 2]
                            pv = pb.t.bitcast(BF16)
                            P.op("pe", lambda e, c=c, pv=pv, sl=ti % 8: e.transpose(pv[0:64, sl * 128:(sl + 1) * 128], srct.t[:, c * 64:(c + 1) * 64], self.cm.t[:, 0, :]),
                                 reads=[srct, self.cm], writes=[pb])
                            ti += 1
                            if ti % 8 == 0:
                                cs_ = c - 7
                                P.op("act", lambda e, pv=pv, cs_=cs_: e.activation(out=dstt.t[:, cs_:cs_ + 8, :], in_=pv[0:64, :].rearrange("p (c f) -> p c f", f=128), func=AF.Copy),
                                     reads=[pb], writes=[dstt])
                    tok_T(v_t, v_tok)
                    for d in range(2):
                        off = 1 if d == 0 else 0
                        Bt = Bp[d].t[:, off:off + MT]
                        Btv = Bt.rearrange("p (c j) -> p c j", j=CH)
                        D1v = D1.t[:, :].rearrange("p (c j) -> p c j", j=CH)
                        P.op("dve", lambda e, d=d: e.tensor_tensor_scan(out=Bp[d].t[:, 1:MT + 1], data0=ones32.t[:, :], data1=lf_t[d].t[:, :], initial=0.0,
                                                                        op0=ALU.mult, op1=(ALU.add if d == 0 else ALU.subtract)),
                             reads=[ones32, lf_t[d]], writes=[Bp[d]])
                        P.op("dve", lambda e, d=d, Btv=Btv: e.tensor_copy(out=r[d].t[:, :], in_=Btv[:, :, CH // 2]), reads=[Bp[d]], writes=[r[d]])
                        if d == 0:
                            P.op("dve", lambda e, d=d: e.tensor_copy(out=rn[d].t[:, 0:NCH - 1], in_=r[d].t[:, 1:NCH]), reads=[r[d]], writes=[rn[d]])
                            P.op("dve", lambda e, d=d: e.tensor_copy(out=rn[d].t[:, NCH - 1:NCH], in_=Bp[d].t[:, MT:MT + 1]), reads=[Bp[d], rn[d]], writes=[rn[d]])
                        else:
                            P.op("dve", lambda e, d=d: e.tensor_copy(out=rn[d].t[:, 1:NCH], in_=r[d].t[:, 0:NCH - 1]), reads=[r[d]], writes=[rn[d]])
                            P.op("dve", lambda e, d=d: e.tensor_copy(out=rn[d].t[:, 0:1], in_=Bp[d].t[:, 0:1]), reads=[Bp[d], rn[d]], writes=[rn[d]])
                        P.op("dve", lambda e, d=d, Btv=Btv, D1v=D1v: e.tensor_tensor(out=D1v, in0=Btv, in1=bcast(r[d].t[:, :], CH), op=ALU.subtract), reads=[Bp[d], r[d]], writes=[D1])
                        P.op("act", lambda e: e.activation(out=E.t[:, :], in_=D1.t[:, :], func=AF.Exp), reads=[D1], writes=[E])
                        P.op("dve", lambda e, d=d: e.tensor_tensor(out=qt[d].t[:, :], in0=q_t.t[:, :], in1=E.t[:, :], op=ALU.mult), reads=[q_t, E], writes=[qt[d]])
                        P.op("act", lambda e: e.activation(out=E.t[:, :], in_=D1.t[:, :], func=AF.Exp, scale=-1.0), reads=[D1], writes=[E])
                        P.op("dve", lambda e, d=d: e.tensor_tensor(out=kt[d].t[:, :], in0=k_t[d].t[:, :], in1=E.t[:, :], op=ALU.mult), reads=[k_t[d], E], writes=[kt[d]])
                        P.op("dve", lambda e, d=d, Btv=Btv, D1v=D1v: e.tensor_tensor(out=D1v, in0=Btv, in1=bcast(rn[d].t[:, :], CH), op=ALU.subtract), reads=[Bp[d], rn[d]], writes=[D1])
                        P.op("act", lambda e: e.activation(out=E.t[:, :], in_=D1.t[:, :], func=AF.Exp, scale=-1.0), reads=[D1], writes=[E])
                        P.op("dve", lambda e, d=d: e.tensor_tensor(out=kh[d].t[:, :], in0=k_t[d].t[:, :], in1=E.t[:, :], op=ALU.mult), reads=[k_t[d], E], writes=[kh[d]])
                        P.op("dve", lambda e, d=d: e.tensor_tensor(out=g[d].t[:, :], in0=rn[d].t[:, :], in1=r[d].t[:, :], op=ALU.subtract), reads=[rn[d], r[d]], writes=[g[d]])
                        P.op("act", lambda e, d=d: e.activation(out=g[d].t[:, :], in_=g[d].t[:, :], func=AF.Exp), reads=[g[d]], writes=[g[d]])
                        if (seg == 1 and d == 0) or (seg == 0 and d == 1):
                            if d == 0:
                                P.op("act", lambda e, Bt=Bt: e.activation(out=E.t[:, :], in_=Bt, func=AF.Exp), reads=[Bp[d]], writes=[E])
                            else:
                                P.op("dve", lambda e, d=d: e.tensor_scalar(out=nb.t[:, :], in0=Bp[d].t[:, MT:MT + 1], scalar1=-1.0, scalar2=None, op0=ALU.mult), reads=[Bp[d]], writes=[nb])
                                P.op("act", lambda e, Bt=Bt: e.activation(out=E.t[:, :], in_=Bt, func=AF.Exp, bias=nb.t[:, 0:1]), reads=[Bp[d], nb], writes=[E])
                            P.op("dve", lambda e: e.tensor_tensor(out=qc.t[:, :], in0=q_t.t[:, :], in1=E.t[:, :], op=ALU.mult), reads=[q_t, E], writes=[qc])
                            qd = self.qcf_d if d == 0 else self.qcb_d
                            P.dma("sp", qd.t[r0:r0 + 128, c0:c0 + MT], qc.t[:, :], qd, reads=[qc])
                        tok_T(kh[d], kh_tok[d])
                    for i in range(NCH):
                        for d in range(2):
                            c = i if d == 0 else NCH - 1 - i
                            fw = 0 if c <= NCH - 1 - c else 1
                            cur = i % 2; nxt = 1 - cur
                            sl = ai % 8; am = ATm[ai % 4]; ai += 1
                            P.op("pe", lambda e, d=d, c=c, sl=sl: e.matmul(pA.t[0:64, sl * 64:(sl + 1) * 64], lhsT=kt[d].t[:, c * 64:(c + 1) * 64], rhs=qt[d].t[:, c * 64:(c + 1) * 64], start=True, stop=True),
                                 reads=[kt[d], qt[d]], writes=[pA])
                            P.op("dve", lambda e, d=d, sl=sl, am=am: e.tensor_tensor(out=am.t[:, :], in0=pA.t[0:64, sl * 64:(sl + 1) * 64], in1=self.cm.t[0:64, 2 + d, 0:64], op=ALU.mult),
                                 reads=[pA, self.cm], writes=[am])
                            po = self.ps[4 + ai % 4]
                            P.op("pe", lambda e, d=d, c=c, am=am, po=po, i=i: e.matmul(po.t[:, 0:64], lhsT=v_tok.t[:, c, :], rhs=am.t[:, :], start=True, stop=(i == 0)),
                                 reads=[v_tok, am], writes=[po])
                            if i > 0:
                                P.op("pe", lambda e, d=d, c=c, po=po, cur=cur: e.matmul(po.t[:, 0:64], lhsT=Mb[d][cur].t[:, :], rhs=qt[d].t[:, c * 64:(c + 1) * 64], start=False, stop=True),
                                     reads=[Mb[d][cur], qt[d]], writes=[po])
                            if d == fw:
                                P.op("act", lambda e, c=c, po=po: e.activation(out=o32.t[:, c * 64:(c + 1) * 64], in_=po.t[:, 0:64], func=AF.Copy), reads=[po], writes=[o32])
                            else:
                                P.op("dve", lambda e, c=c, po=po: e.tensor_tensor(out=o32.t[:, c * 64:(c + 1) * 64], in0=o32.t[:, c * 64:(c + 1) * 64], in1=po.t[:, 0:64], op=ALU.add), reads=[po, o32], writes=[o32])
                            ul = ui % 4; ui += 1
                            P.op("pe", lambda e, d=d, c=c, ul=ul: e.matmul(pU.t[:, ul * 128:(ul + 1) * 128], lhsT=kh_tok[d].t[:, c, :], rhs=v_tok.t[:, c, :], start=True, stop=True),
                                 reads=[kh_tok[d], v_tok], writes=[pU])
                            if i == 0:
                                P.op("dve", lambda e, d=d, ul=ul, nxt=nxt: e.tensor_copy(out=M32[d][nxt].t[:, :], in_=pU.t[:, ul * 128:(ul + 1) * 128]), reads=[pU], writes=[M32[d][nxt]])
                            else:
                                P.op("dve", lambda e, d=d, c=c, ul=ul, cur=cur, nxt=nxt: e.scalar_tensor_tensor(out=M32[d][nxt].t[:, :], in0=M32[d][cur].t[:, :], scalar=g[d].t[:, c:c + 1], in1=pU.t[:, ul * 128:(ul + 1) * 128], op0=ALU.mult, op1=ALU.add),
                                     reads=[M32[d][cur], g[d], pU], writes=[M32[d][nxt]])
                            P.op("act", lambda e, d=d, nxt=nxt: e.activation(out=Mb[d][nxt].t[:, :], in_=M32[d][nxt].t[:, :], func=AF.Copy), reads=[M32[d][nxt]], writes=[Mb[d][nxt]])
                    fin = NCH % 2
                    for d in range(2):
                        P.dma("sp", self.S_d.t[h, seg, d], M32[d][fin].t[:, :], self.S_d, reads=[M32[d][fin]])
                    P.dma("sp", self.o_d.t[r0:r0 + 128, c0:c0 + MT], o32.t[:, :], self.o_d, reads=[o32])
        with P.phase():
            fo32 = [P.sb("fo%d" % i, [128, MT], F32) for i in range(2)]
            gt = [P.sb("fg%d" % i, [128, MT], BF16) for i in range(2)]
            fqc = [P.sb("fq%d" % i, [128, MT], BF16) for i in range(2)]
            Sr = [P.sb("fS%d" % i, [128, 128], F32) for i in range(2)]
            Sb = [P.sb("fSb%d" % i, [128, 128], BF16) for i in range(2)]
            fsq = P.sb("fsq", [128, MT], BF16)
            frst = P.sb("frs", [128, MT], F32)
            og = [P.sb("fog%d" % i, [128, MT], BF16) for i in range(2)]
            flag = P.spcol(SP_FLAG)
            it = 0
            for h in range(NH):
                r0 = h * 128
                for seg in range(2):
                    s_ = it % 2; it += 1
                    c0 = seg * MT
                    P.dma("sp", fo32[s_].t[:], self.o_d.t[r0:r0 + 128, c0:c0 + MT], fo32[s_], reads=[self.o_d])
                    P.dma("sp", gt[s_].t[:], self.gs_d.t[r0:r0 + 128, c0:c0 + MT], gt[s_], reads=[self.gs_d])
                    qd = self.qcf_d if seg == 1 else self.qcb_d
                    P.dma("sp", fqc[s_].t[:], qd.t[r0:r0 + 128, c0:c0 + MT], fqc[s_], reads=[qd])
                    P.dma("sp", Sr[s_].t[:], self.S_d.t[h, 1 - seg, 0 if seg == 1 else 1], Sr[s_], reads=[self.S_d])
                    P.op("dve", lambda e, s_=s_: e.tensor_scalar(out=Sb[s_].t[:, :], in0=Sr[s_].t[:, :], scalar1=flag, scalar2=None, op0=ALU.mult), reads=[Sr[s_], self.spt], writes=[Sb[s_]])
                    for tb in range(MT // 512):
                        pc = self.ps[4 + tb]
                        P.op("pe", lambda e, s_=s_, tb=tb, pc=pc: e.matmul(pc.t[:, :], lhsT=Sb[s_].t[:, :], rhs=fqc[s_].t[:, tb * 512:(tb + 1) * 512], start=True, stop=True), reads=[Sb[s_], fqc[s_]], writes=[pc])
                        P.op("dve", lambda e, s_=s_, tb=tb, pc=pc: e.tensor_tensor(out=fo32[s_].t[:, tb * 512:(tb + 1) * 512], in0=fo32[s_].t[:, tb * 512:(tb + 1) * 512], in1=pc.t[:, :], op=ALU.add),
                             reads=[fo32[s_], pc], writes=[fo32[s_]])
                    P.op("act", lambda e, s_=s_: e.activation(out=fsq.t[:, :], in_=fo32[s_].t[:, :], func=AF.Square), reads=[fo32[s_]], writes=[fsq])
                    for tb in range(MT // 512):
                        pq = self.ps[tb % 4]
                        P.op("pe", lambda e, tb=tb, pq=pq: e.matmul(pq.t[:, :], lhsT=self.ones.t[:, :], rhs=fsq.t[:, tb * 512:(tb + 1) * 512], start=True, stop=True), reads=[self.ones, fsq], writes=[pq])
                        P.op("act", lambda e, tb=tb, pq=pq: e.activation(out=frst.t[:, tb * 512:(tb + 1) * 512], in_=pq.t[:, :], func=AF.Sqrt, scale=1.0 / 128.0, bias=self.epsc.t[:, 0:1]), reads=[pq, self.epsc], writes=[frst])
                    P.op("dve", lambda e: e.reciprocal(out=frst.t[:, :], in_=frst.t[:, :]), reads=[frst], writes=[frst])
                    P.op("dve", lambda e, s_=s_: e.tensor_tensor(out=fo32[s_].t[:, :], in0=fo32[s_].t[:, :], in1=frst.t[:, :], op=ALU.mult), reads=[fo32[s_], frst], writes=[fo32[s_]])
                    P.op("dve", lambda e, s_=s_: e.scalar_tensor_tensor(out=og[s_].t[:, :], in0=fo32[s_].t[:, :], scalar=P.spcol(SP_HGN + j), in1=gt[s_].t[:, :], op0=ALU.mult, op1=ALU.mult),
                         reads=[fo32[s_], gt[s_], self.spt], writes=[og[s_]])
                    P.dma("sp", self.oT_d.t[r0:r0 + 128, c0:c0 + MT], og[s_].t[:, :], self.oT_d, reads=[og[s_]])
        P.out_proj(self.hg_w_out.t[j], 1 * 4 + l, src, dst)


    def attn(self, l, src, dst):
        P, nc, T, MT = self, self.nc, self.T, self.MT
        j = l // 2
        lam_init = 0.8 - 0.6 * math.exp(-0.3 * l)
        P.norm_phase(src, 0 * 4 + l)
        TP = min(T, 1024)
        NK = T // 128
        hv = self.hT_d.t.rearrange("(k p) t -> p k t", p=128)
        if getattr(self, "skip_proj", False):
            return self._attn_core(l, src, dst)
        with P.phase():
            hT = P.sb("ahT", [128, KC, TP], BF16)
            cst = P.sb("acs", [128, 2, TP], F32)
            Wv = P.sb("aWv", [128, KC, D], BF16)
            for q in range(KC):
                P.dma("pool", Wv.t[:, q, :], self.da_w_v.t[j, :, q, :], Wv)
            wq = [P.sb("awq%d" % i, [128, KC, 128], BF16) for i in range(3)]
            xb = [P.sb("axb%d" % i, [128, 512], BF16) for i in range(2)]
            t1 = [P.sb("at1%d" % i, [128, 512], F32) for i in range(2)]
            t2 = [P.sb("at2%d" % i, [128, 512], F32) for i in range(2)]
            ob = [P.sb("aob%d" % i, [128, 512], BF16) for i in range(3)]
            vb = [P.sb("avb%d" % i, [128, D], BF16) for i in range(2)]
            wi = 0; oi = 0; it = 0
            for p in range(T // TP):
                P.dma("sp", hT.t[:], hv[:, :, 1 + p * TP:1 + (p + 1) * TP], hT, reads=[self.hT_d])
                P.dma("sp", cst.t[:], self.cs.t[:, :, p * TP:(p + 1) * TP], cst)
                for mch in range(32):
                    w = wq[wi % 3]; wi += 1
                    P.dma("pool", w.t[:], self.da_w_qk.t[j, mch], w)
                    dstd = self.qT_d if mch < 16 else self.kT_d
                    for tb in range(TP // 512):
                        pm = self.ps[it % 2]; pr = self.ps[2 + it % 2]; sl = it % 2; it += 1
                        for k in range(KC):
                            P.op("pe", lambda e, w=w, k=k, tb=tb, pm=pm: e.matmul(pm.t[:, :], lhsT=w.t[:, k, :], rhs=hT.t[:, k, tb * 512:(tb + 1) * 512], start=(k == 0), stop=(k == KC - 1)),
                                 reads=[w, hT], writes=[pm])
                        o = ob[oi % 3]; oi += 1
                        P.op("act", lambda e, sl=sl, pm=pm: e.activation(out=xb[sl].t[:, :], in_=pm.t[:, :], func=AF.Copy), reads=[pm], writes=[xb[sl]])
                        P.op("pe", lambda e, sl=sl, pr=pr: e.matmul(pr.t[:, :], lhsT=self.cm.t[:, 1, :], rhs=xb[sl].t[:, :], start=True, stop=True),
                             reads=[xb[sl], self.cm], writes=[pr])
                        P.op("act", lambda e, sl=sl, pm=pm: e.activation(out=t1[sl].t[:, :], in_=pm.t[:, :], func=AF.Copy), reads=[pm], writes=[t1[sl]])
                        P.op("dve", lambda e, sl=sl, tb=tb: e.tensor_tensor(out=t1[sl].t[:, :], in0=t1[sl].t[:, :], in1=cst.t[:, 0, tb * 512:(tb + 1) * 512], op=ALU.mult),
                             reads=[t1[sl], cst], writes=[t1[sl]])
                        P.op("dve", lambda e, sl=sl, pr=pr, tb=tb: e.tensor_tensor(out=t2[sl].t[:, :], in0=pr.t[:, :], in1=cst.t[:, 1, tb * 512:(tb + 1) * 512], op=ALU.mult),
                             reads=[pr, cst], writes=[t2[sl]])
                        P.op("dve", lambda e, sl=sl, o=o: e.tensor_tensor(out=o.t[:, :], in0=t1[sl].t[:, :], in1=t2[sl].t[:, :], op=ALU.add),
                             reads=[t1[sl], t2[sl]], writes=[o])
                        r0 = (mch % 16) * 128
                        c0 = p * TP + tb * 512
                        P.dma("sp", dstd.t[r0:r0 + 128, c0:c0 + 512], o.t[:, :], dstd, reads=[o])
                for tt in range(TP // 128):
                    v = vb[tt % 2]
                    for cb in range(4):
                        pd = self.ps[4 + cb]
                        for k in range(KC):
                            P.op("pe", lambda e, k=k, tt=tt, cb=cb, pd=pd: e.matmul(pd.t[:, :], lhsT=hT.t[:, k, tt * 128:(tt + 1) * 128], rhs=Wv.t[:, k, cb * 512:(cb + 1) * 512], start=(k == 0), stop=(k == KC - 1)),
                                 reads=[hT, Wv], writes=[pd])
                        if cb % 2 == 0:
                            P.op("act", lambda e, v=v, cb=cb, pd=pd: e.activation(out=v.t[:, cb * 512:(cb + 1) * 512], in_=pd.t[:, :], func=AF.Copy), reads=[pd], writes=[v])
                        else:
                            P.op("dve", lambda e, v=v, cb=cb, pd=pd: e.tensor_copy(out=v.t[:, cb * 512:(cb + 1) * 512], in_=pd.t[:, :]), reads=[pd], writes=[v])
                    tr = p * TP + tt * 128
                    P.dma("sp", self.v_d.t[tr:tr + 128, :], v.t[:, :], self.v_d, reads=[v])
        return self._attn_core(l, src, dst)

    def _attn_core(self, l, src, dst):
        P, nc, T, MT = self, self.nc, self.T, self.MT
        j = l // 2
        lam_init = 0.8 - 0.6 * math.exp(-0.3 * l)
        NK = T // 128
        if getattr(self, "skip_core", False):
            return P.out_proj(self.da_w_out.t[j], 1 * 4 + l, src, dst)
        with P.phase():
            mb = P.sb("amb", [128, 2, NK], F32)
            P.dma("sp", mb.t[:], self.maskb.t, mb)
            pr_f = P.sb("apr", [128, 2], F32); pr_b = P.sb("aprb", [128, 2], BF16)
            ex = P.sb("aex", [128, 2], F32); nl = P.sb("anl", [128, 1], F32)
            lc = SP_LAM + j * 4
            P.op("dve", lambda e: e.tensor_tensor(out=pr_f.t[:, 0:1], in0=P.spcol(lc + 0), in1=P.spcol(lc + 1), op=ALU.mult), reads=[self.spt], writes=[pr_f])
            P.op("dve", lambda e: e.tensor_tensor(out=pr_f.t[:, 1:2], in0=P.spcol(lc + 2), in1=P.spcol(lc + 3), op=ALU.mult), reads=[self.spt, pr_f], writes=[pr_f])
            P.op("dve", lambda e: e.tensor_copy(out=pr_b.t[:, :], in_=pr_f.t[:, :]), reads=[pr_f], writes=[pr_b])
            P.op("pe", lambda e: e.matmul(self.ps[0].t[:, 0:2], lhsT=self.ones.t[:, :], rhs=pr_b.t[:, :], start=True, stop=True), reads=[self.ones, pr_b], writes=[self.ps[0]])
            P.op("act", lambda e: e.activation(out=ex.t[:, :], in_=self.ps[0].t[:, 0:2], func=AF.Exp), reads=[self.ps[0]], writes=[ex])
            P.op("dve", lambda e: e.tensor_tensor(out=nl.t[:, :], in0=ex.t[:, 1:2], in1=ex.t[:, 0:1], op=ALU.subtract), reads=[ex], writes=[nl])
            P.op("dve", lambda e: e.tensor_scalar(out=nl.t[:, :], in0=nl.t[:, :], scalar1=-lam_init, scalar2=None, op0=ALU.add), reads=[nl], writes=[nl])
            KT = [P.sb("aKT%d" % i, [128, 2, T], BF16) for i in range(2)]
            QT = [P.sb("aQT%d" % i, [128, 2, T], BF16) for i in range(2)]
            Vh = [P.sb("aVh%d" % i, [128, NK, 256], BF16) for i in range(2)]
            PT = [P.sb("aPT%d" % i, [128, 512], BF16) for i in range(3)]
            rz = [P.sb("arz%d" % i, [128, 512], F32) for i in range(2)]
            ta = P.sb("ata", [128, 512], F32); tb_ = P.sb("atb", [128, 512], F32)
            oo = [P.sb("aoo%d" % i, [128, 512], F32) for i in range(2)]
            sq = [P.sb("asq%d" % i, [128, 512], BF16) for i in range(2)]
            rst = P.sb("arst", [128, 512], F32)
            onb = [P.sb("aonb%d" % i, [128, 512], BF16) for i in range(4)]
            vv = self.v_d.t.rearrange("(kt p) c -> p kt c", p=128)
            scale = 128.0 ** -0.5
            pti = 0; oni = 0; sti = 0
            def load_head(h):
                hs = h % 2
                for s_ in range(2):
                    r0 = (2 * h + s_) * 128
                    P.dma("sp", KT[hs].t[:, s_, :], self.kT_d.t[r0:r0 + 128, :], KT[hs], reads=[self.kT_d])
                    P.dma("sp", QT[hs].t[:, s_, :], self.qT_d.t[r0:r0 + 128, :], QT[hs], reads=[self.qT_d])
                P.dma("sp", Vh[hs].t[:], vv[:, :, h * 256:(h + 1) * 256], Vh[hs], reads=[self.v_d])

            load_head(0)
            for h in range(8):
                hs = h % 2
                if h + 1 < 8:
                    load_head(h + 1)
                for qt in range(T // 512):
                    seg = (qt * 512) // MT
                    iters = [(s_, kt) for s_ in range(2) for kt in range(NK)]
                    slots = {}

                    def emit_S(i, hs=hs, qt=qt, seg=seg):
                        nonlocal sti, pti
                        s_, kt = iters[i]
                        pS = self.ps[sti % 2]; sti += 1
                        pt = PT[pti % 3]; pti += 1
                        slots[i] = pt
                        P.op("pe", lambda e, hs=hs, s_=s_, kt=kt, qt=qt, pS=pS: e.matmul(pS.t[:, :], lhsT=KT[hs].t[:, s_, kt * 128:(kt + 1) * 128], rhs=QT[hs].t[:, s_, qt * 512:(qt + 1) * 512], start=True, stop=True),
                             reads=[KT[hs], QT[hs]], writes=[pS])
                        P.op("act", lambda e, pt=pt, pS=pS, seg=seg, kt=kt: e.activation(out=pt.t[:, :], in_=pS.t[:, :], func=AF.Exp, scale=scale, bias=mb.t[:, seg, kt:kt + 1]),
                             reads=[pS, mb], writes=[pt])

                    emit_S(0)
                    for i, (s_, kt) in enumerate(iters):
                        if i + 1 < len(iters):
                            emit_S(i + 1)
                        pt = slots.pop(i)
                        pO = [self.ps[2 + 3 * s_], self.ps[3 + 3 * s_]]; pZ = self.ps[4 + 3 * s_]
                        for half in range(2):
                            P.op("pe", lambda e, hs=hs, kt=kt, half=half, pt=pt, pO=pO: e.matmul(pO[half].t[:, :], lhsT=Vh[hs].t[:, kt, half * 128:(half + 1) * 128], rhs=pt.t[:, :], start=(kt == 0), stop=(kt == NK - 1)),
                                 reads=[Vh[hs], pt], writes=[pO[half]])
                        P.op("pe", lambda e, kt=kt, pt=pt, pZ=pZ: e.matmul(pZ.t[:, :], lhsT=self.ones.t[:, :], rhs=pt.t[:, :], start=(kt == 0), stop=(kt == NK - 1)),
                             reads=[self.ones, pt], writes=[pZ])
                        if kt == NK - 1:
                            P.op("dve", lambda e, s_=s_, pZ=pZ: e.reciprocal(out=rz[s_].t[:, :], in_=pZ.t[:, :]), reads=[pZ], writes=[rz[s_]])
                    pq = self.ps[0]
                    for half in range(2):
                        P.op("dve", lambda e, half=half: e.tensor_tensor(out=ta.t[:, :], in0=self.ps[2 + half].t[:, :], in1=rz[0].t[:, :], op=ALU.mult), reads=[self.ps[2 + half], rz[0]], writes=[ta])
                        P.op("dve", lambda e, half=half: e.tensor_tensor(out=tb_.t[:, :], in0=self.ps[5 + half].t[:, :], in1=rz[1].t[:, :], op=ALU.mult), reads=[self.ps[5 + half], rz[1]], writes=[tb_])
                        P.op("dve", lambda e, half=half: e.scalar_tensor_tensor(out=oo[half].t[:, :], in0=tb_.t[:, :], scalar=nl.t[:, 0:1], in1=ta.t[:, :], op0=ALU.mult, op1=ALU.add),
                             reads=[ta, tb_, nl], writes=[oo[half]])
                        P.op("act", lambda e, half=half: e.activation(out=sq[half].t[:, :], in_=oo[half].t[:, :], func=AF.Square), reads=[oo[half]], writes=[sq[half]])
                        P.op("pe", lambda e, half=half, pq=pq: e.matmul(pq.t[:, :], lhsT=self.ones.t[:, :], rhs=sq[half].t[:, :], start=(half == 0), stop=(half == 1)),
                             reads=[self.ones, sq[half]], writes=[pq])
                    P.op("act", lambda e, pq=pq: e.activation(out=rst.t[:, :], in_=pq.t[:, :], func=AF.Sqrt, scale=1.0 / 256.0, bias=self.epsc.t[:, 0:1]), reads=[pq, self.epsc], writes=[rst])
                    P.op("dve", lambda e: e.reciprocal(out=rst.t[:, :], in_=rst.t[:, :]), reads=[rst], writes=[rst])
                    for half in range(2):
                        on = onb[oni % 4]; oni += 1
                        P.op("dve", lambda e, half=half: e.tensor_tensor(out=oo[half].t[:, :], in0=oo[half].t[:, :], in1=rst.t[:, :], op=ALU.mult), reads=[oo[half], rst], writes=[oo[half]])
                        gcol = P.spcol(SP_SUBLN + j * 2 + half)
                        P.op("dve", lambda e, half=half, on=on, gcol=gcol: e.tensor_scalar(out=on.t[:, :], in0=oo[half].t[:, :], scalar1=gcol, scalar2=(1.0 - lam_init), op0=ALU.mult, op1=ALU.mult),
                             reads=[oo[half], self.spt], writes=[on])
                        r0 = h * 256 + half * 128
                        P.dma("sp", self.oT_d.t[r0:r0 + 128, qt * 512:(qt + 1) * 512], on.t[:, :], self.oT_d, reads=[on])
        if getattr(self, "skip_oproj", False):
            return P.out_proj_dummy(src, dst)
        P.out_proj(self.da_w_out.t[j], 1 * 4 + l, src, dst)

    def out_proj_dummy(self, src, dst):
        P = self
        with P.phase():
            t = P.sb("dmy", [128, D], F32)
            for i in range(self.NTT):
                P.dma("sp", t.t[:], src.t[i * 128:(i + 1) * 128, :], t, reads=[src])
                P.dma("sp", dst.t[i * 128:(i + 1) * 128, :], t.t[:], dst, reads=[t])


def prep_shared(inp, T, need=("hg", "da", "ffn")):
    f = lambda a: np.ascontiguousarray(np.asarray(a, dtype=np.float32))
    out = {}
    g = np.stack([f(inp["pre_mix_g"]), f(inp["post_mix_g"]), f(inp["pre_ffn_g"]), f(inp["post_ffn_g"])], 0).reshape(16, 1, D)
    out["gains"] = np.ascontiguousarray(np.broadcast_to(g, (16, 128, D)))

    def formA(w, ncols):
        return np.ascontiguousarray(w.reshape(KC, 128, ncols // 128, 128).transpose(2, 1, 0, 3))

    def formB(w):
        K, N = w.shape
        return np.ascontiguousarray(w.reshape(K // 128, 128, N).transpose(1, 0, 2))

    if "hg" in need:
        out["hg_w_in"] = np.stack([formA(f(inp["hg_w_in"][j]), 5 * D) for j in range(2)], 0)
        out["hg_w_out"] = np.stack([formB(f(inp["hg_w_out"][j])) for j in range(2)], 0)
    if "da" in need:
        qkv = f(inp["da_w_qkv"])
        out["da_w_qk"] = np.stack([formA(qkv[j][:, :2 * D], 2 * D) for j in range(2)], 0)
        out["da_w_v"] = np.stack([formB(qkv[j][:, 2 * D:]) for j in range(2)], 0)
        out["da_w_out"] = np.stack([formB(f(inp["da_w_out"][j])) for j in range(2)], 0)
    if "ffn" in need:
        out["ffn_w_up"] = np.stack([formA(f(inp["ffn_w_up"][l]), 2 * DFF) for l in range(4)], 0)
        out["ffn_w_down"] = np.stack([formB(f(inp["ffn_w_down"][l])) for l in range(4)], 0)
    sp = np.zeros((128, NSP), np.float32)
    cw = f(inp["ffn_conv_w"])
    sp[:, SP_CONVW:SP_CONVW + 4 * 3 * 88] = cw.reshape(4, 3, 88, 128).transpose(3, 0, 1, 2).reshape(128, -1)
    cb = f(inp["ffn_conv_b"])
    sp[:, SP_CONVB:SP_CONVB + 4 * 88] = cb.reshape(4, 88, 128).transpose(2, 0, 1).reshape(128, -1)
    lb = f(inp["hg_lower_bounds"])
    sp[:, SP_LB:SP_LB + 64] = lb.reshape(4, 16, 128).transpose(2, 0, 1).reshape(128, -1)
    sp[:, SP_HGN:SP_HGN + 2] = f(inp["hg_norm_g"]).T
    sp[:, SP_SUBLN:SP_SUBLN + 4] = f(inp["da_subln_g"]).reshape(2, 2, 128).transpose(2, 0, 1).reshape(128, -1)
    sp[:, SP_LAM:SP_LAM + 8] = f(inp["da_lambda"]).transpose(2, 0, 1).reshape(128, -1)
    out["spd"] = sp
    cm = np.zeros((128, 4, 128), np.float32)
    cm[:, 0, :] = np.eye(128)
    for m in range(16):
        cm[m + 16, 1, m] = -1.0
        cm[m, 1, m + 16] = 1.0
    s_idx = np.arange(64)[:, None]; t_idx = np.arange(64)[None, :]
    cm[:64, 2, :64] = (s_idx <= t_idx)
    cm[:64, 3, :64] = (s_idx >= t_idx)
    out["cmat"] = cm.astype(BF)
    return out


def prep_core(x_core, coupled, T, shared):
    m = dict(shared)
    m["x_in"] = np.ascontiguousarray(x_core, dtype=np.float32)
    sp = shared["spd"].copy()
    sp[:, SP_FLAG] = 1.0 if coupled else 0.0
    m["spd"] = sp
    MT = T // 2
    pos = np.arange(T) if coupled else (np.arange(T) % MT)
    half = 16
    inv = 1.0 / (ROPE_THETA ** (np.arange(0, 32, 2, dtype=np.float32) / 32.0))
    ang = pos.astype(np.float32)[None, :] * inv.astype(np.float32)[:, None]
    cs = np.zeros((128, 2, T), np.float32)
    cs[:, 0] = 1.0
    cs[:16, 0] = np.cos(ang); cs[16:32, 0] = np.cos(ang)
    cs[:16, 1] = np.sin(ang); cs[16:32, 1] = np.sin(ang)
    m["cs"] = cs
    nk = T // 128
    mb = np.zeros((128, 2, nk), np.float32)
    if not coupled:
        mb[:, 0, nk // 2:] = -30000.0
        mb[:, 1, :nk // 2] = -30000.0
    m["maskb"] = mb
    return m


_CACHE = {}


def get_prog(T, layers=(0, 1, 2, 3), parts=("mix", "ffn"), **opts):
    key = (T, tuple(layers), tuple(parts), tuple(sorted(opts.items())))
    if key not in _CACHE:
        p = Prog(T, layers, parts)
        for k, v in opts.items():
            setattr(p, k, v)
        _CACHE[key] = p.build()
    return _CACHE[key]


def kernel(**inputs):
    T = 4096
    xp = np.asarray(inputs["x_prompt"], dtype=np.float32)
    xs = np.asarray(inputs["x_sample"], dtype=np.float32)
    shared = prep_shared(inputs, T)
    cores = [
        (xp[0:2].reshape(T, D), False),
        (xp[2:4].reshape(T, D), False),
        (xs[0], True),
        (xs[1], True),
    ]
    maps = [prep_core(x, c, T, shared) for (x, c) in cores]
    zero = {k: np.zeros_like(v) for k, v in maps[0].items()}
    active = [0, 1, 4, 5]
    in_maps = [zero] * 8
    in_maps = list(in_maps)
    for a, m in zip(active, maps):
        in_maps[a] = m
    nc = get_prog(T)
    res = run_bass_kernel_spmd(nc, in_maps, core_ids=list(range(8)))
    ys = [np.asarray(res.results[i]["y"], dtype=np.float32) for i in active]
    y_prompt = np.concatenate([ys[0].reshape(2, 2048, D), ys[1].reshape(2, 2048, D)], 0)
    y_sample = np.stack([ys[2], ys[3]], 0)
    return (y_prompt, y_sample)
```

```python
import contextlib
import math
import numpy as np
import ml_dtypes
import concourse.bass as bass
import concourse.mybir as mybir
from concourse.bass_utils import run_bass_kernel_spmd

F32 = mybir.dt.float32
BF16 = mybir.dt.bfloat16
U8 = mybir.dt.uint8
AF = mybir.ActivationFunctionType
ALU = mybir.AluOpType
BF = ml_dtypes.bfloat16

D = 2048
KC = 16
DFF = 5632
FC = 44
NH = 16
CH = 64
EPS = 1e-6
DEPTH = 4
ROPE_THETA = 500000.0

ENGS = ("pe", "act", "dve", "pool", "sp")
ENGOBJ = {"pe": "tensor", "act": "scalar", "dve": "vector", "pool": "gpsimd", "sp": "sync"}


class Buf:
    def __init__(self, name):
        self.name = name
        self.last_write = None
        self.reads = []
        self.sem = None
        self.dma_count = 0
        self.sem_id = None
        self.base = 0


class Instr:
    __slots__ = ("eng", "fn", "waits", "marked", "idx", "is_dma", "dbuf", "dval")

    def __init__(self, eng, fn, is_dma=False):
        self.eng = eng; self.fn = fn; self.waits = []; self.marked = False
        self.idx = None; self.is_dma = is_dma; self.dbuf = None; self.dval = None


class Sched:
    def __init__(self, nc, same_engine_sync=True):
        self.nc = nc
        self.streams = {e: [] for e in ENGS}
        self.waited = {e: {} for e in ENGS}
        self.same = same_engine_sync
        self.bufs = []
        self.free_sems = []
        self.nsem = 0
        self.phase_bufs = None

    def buf(self, name):
        b = Buf(name); self.bufs.append(b)
        if self.phase_bufs is not None:
            self.phase_bufs.append(b)
        return b

    def release(self, bufs):
        for b in bufs:
            if b.sem_id is not None:
                self.free_sems.append((b.sem_kind, b.sem_id, b.base + b.dma_count))

    def _dep(self, ins, dep):
        if dep is None or dep is ins:
            return
        if dep.is_dma:
            key = ("dma", id(dep.dbuf)); val = dep.dval
        else:
            if dep.eng == ins.eng and (dep.eng == "pe" or not self.same):
                return
            key = ("eng", dep.eng); val = dep.idx
        w = self.waited[ins.eng]
        if w.get(key, -1) >= val:
            return
        w[key] = val
        dep.marked = True
        ins.waits.append(dep)

    def op(self, eng, fn, reads=(), writes=(), dma_dst=None):
        is_dma = dma_dst is not None
        ins = Instr(eng, fn, is_dma)
        st = self.streams[eng]
        ins.idx = len(st)
        if is_dma:
            if dma_dst.sem_id is None:
                kind = "sw" if eng == "pool" else "hw"
                dma_dst.sem_kind = kind
                cand = [x for x in self.free_sems if x[0] == kind]
                if cand:
                    x = cand[-1]
                    self.free_sems.remove(x)
                    dma_dst.sem_id, dma_dst.base = x[1], x[2]
                else:
                    dma_dst.sem_id, dma_dst.base = self.nsem, 0
                    self.nsem += 1
            dma_dst.dma_count += 1
            ins.dbuf = dma_dst; ins.dval = dma_dst.dma_count
            dma_dst.last_dma = ins
        for b in reads:
            self._dep(ins, b.last_write)
        for b in writes:
            self._dep(ins, b.last_write)
            for r in b.reads:
                self._dep(ins, r)
        for b in writes:
            b.last_write = ins; b.reads = []
        for b in reads:
            b.reads.append(ins)
        st.append(ins)
        return ins

    def barrier(self):
        last = {}
        for e in ENGS:
            for ins in reversed(self.streams[e]):
                if (not ins.is_dma) and ins.fn is not None:
                    last[e] = ins
                    break
        dmas = [b.last_dma for b in self.bufs if getattr(b, "last_dma", None) is not None]
        news = []
        for e in ENGS:
            ins = Instr(e, None)
            ins.idx = len(self.streams[e])
            for e2 in ENGS:
                if e2 in last:
                    self._dep(ins, last[e2])
            for d in dmas:
                self._dep(ins, d)
            news.append(ins)
        for ins in news:
            self.streams[ins.eng].append(ins)
        for b in self.bufs:
            b.last_dma = None

    def emit(self, es, final_bufs=()):
        nc = self.nc
        esem = {e: es.enter_context(nc.semaphore("s_" + e)) for e in ENGS}
        dsems = [es.enter_context(nc.semaphore("d%d" % i)) for i in range(self.nsem)]
        for b in self.bufs:
            if b.sem_id is not None:
                b.sem = dsems[b.sem_id]
        cnt = {}
        for e in ENGS:
            c = 0
            for ins in self.streams[e]:
                if (not ins.is_dma) and ins.marked:
                    c += 1
                    cnt[id(ins)] = c
        block = es.enter_context(nc.Block())

        def run(e, eng):
            for ins in self.streams[e]:
                for d in ins.waits:
                    if d.is_dma:
                        eng.wait_ge(d.dbuf.sem, 16 * (d.dbuf.base + d.dval))
                    else:
                        eng.wait_ge(esem[d.eng], cnt[id(d)])
                if ins.fn is None:
                    continue
                h = ins.fn(eng)
                if ins.is_dma:
                    h.then_inc(ins.dbuf.sem, 16)
                elif ins.marked:
                    h.then_inc(esem[e], 1)
            if e == "sp":
                for b in final_bufs:
                    if b.dma_count:
                        eng.wait_ge(b.sem, 16 * (b.base + b.dma_count))

        for e in ENGS:
            getattr(block, ENGOBJ[e])(lambda eng, e=e: run(e, eng))


class TL:
    def __init__(self, t, b):
        self.t = t; self.b = b


SP_CONVW = 0
SP_CONVB = SP_CONVW + 4 * 3 * 88
SP_LB = SP_CONVB + 4 * 88
SP_HGN = SP_LB + 64
SP_SUBLN = SP_HGN + 2
SP_LAM = SP_SUBLN + 4
SP_FLAG = SP_LAM + 8
NSP = SP_FLAG + 1


def needed(layers, parts):
    need = set()
    if "ffn" in parts:
        need.add("ffn")
    if "mix" in parts:
        for l in layers:
            need.add("hg" if l % 2 == 0 else "da")
    return need


WNAMES = {"hg": ("hg_w_in", "hg_w_out"), "da": ("da_w_qk", "da_w_v", "da_w_out"), "ffn": ("ffn_w_up", "ffn_w_down")}


def bcast(ap, n):
    return bass.AP(ap.tensor, ap.offset, [list(x) for x in ap.ap] + [[0, n]])


class Prog:
    def __init__(self, T, layers=(0, 1, 2, 3), parts=("mix", "ffn")):
        self.T = T
        self.MT = T // 2
        self.NTT = T // 128
        self.layers = layers
        self.parts = parts
        self.nc = bass.Bass("TRN2", target_bir_lowering=False)
        self.S = Sched(self.nc)
        self.es = contextlib.ExitStack()
        self.uid = 0

    def sb(self, name, shape, dt):
        self.uid += 1
        t = self.es.enter_context(self.nc.sbuf_tensor("%s_%d" % (name, self.uid), list(shape), dt))
        return TL(t, self.S.buf(name))

    def dram(self, name, shape, dt, kind="Internal"):
        t = self.nc.dram_tensor(name, list(shape), dt, kind=kind).ap()
        return TL(t, self.S.buf(name))

    def op(self, eng, fn, reads=(), writes=(), dma_dst=None):
        return self.S.op(eng, fn, [r.b for r in reads], [w.b for w in writes], None if dma_dst is None else dma_dst.b)

    def dma(self, eng, out_ap, in_ap, dst, reads=(), extra_writes=(), **kw):
        return self.op(eng, lambda e: e.dma_start(out=out_ap, in_=in_ap, **kw), reads=reads,
                       writes=[dst] + list(extra_writes), dma_dst=dst)

    @contextlib.contextmanager
    def phase(self):
        old = self.es
        self.S.phase_bufs = []
        with contextlib.ExitStack() as es:
            self.es = es
            yield
        self.es = old
        self.S.barrier()
        pb = self.S.phase_bufs
        self.S.phase_bufs = None
        self.S.release(pb)

    def build(self):
        nc, T, MT = self.nc, self.T, self.MT
        P = self
        self.x_in = P.dram("x_in", [T, D], F32, "ExternalInput")
        self.y = P.dram("y", [T, D], F32, "ExternalOutput")
        self.xs = P.dram("xs", [T, D], F32)
        self.gains = P.dram("gains", [16, 128, D], F32, "ExternalInput")
        self.spd = P.dram("spd", [128, NSP], F32, "ExternalInput")
        self.cmat = P.dram("cmat", [128, 4, 128], BF16, "ExternalInput")
        self.cs = P.dram("cs", [128, 2, T], F32, "ExternalInput")
        self.maskb = P.dram("maskb", [128, 2, T // 128], F32, "ExternalInput")
        need = needed(self.layers, self.parts)
        if "hg" in need:
            self.hg_w_in = P.dram("hg_w_in", [2, 80, 128, KC, 128], F32, "ExternalInput")
            self.hg_w_out = P.dram("hg_w_out", [2, 128, KC, D], F32, "ExternalInput")
            for nm in ("qs_d", "gs_d", "vT_d", "kf_d", "kb_d", "qcf_d", "qcb_d"):
                setattr(self, nm, P.dram(nm, [D, T], BF16))
            for nm in ("lf_d", "lb_d", "o_d"):
                setattr(self, nm, P.dram(nm, [D, T], F32))
            self.S_d = P.dram("S_d", [NH, 2, 2, 128, 128], F32)
        if "da" in need:
            self.da_w_qk = P.dram("da_w_qk", [2, 32, 128, KC, 128], F32, "ExternalInput")
            self.da_w_v = P.dram("da_w_v", [2, 128, KC, D], F32, "ExternalInput")
            self.da_w_out = P.dram("da_w_out", [2, 128, KC, D], F32, "ExternalInput")
            self.qT_d = P.dram("qT_d", [D, T], BF16)
            self.kT_d = P.dram("kT_d", [D, T], BF16)
            self.v_d = P.dram("v_d", [T, D], BF16)
        if "ffn" in need:
            self.ffn_w_up = P.dram("ffn_w_up", [4, 88, 128, KC, 128], F32, "ExternalInput")
            self.ffn_w_down = P.dram("ffn_w_down", [4, 128, FC, D], F32, "ExternalInput")
            self.wc_up = P.dram("wc_up", [88, 128, KC, 128], BF16)
            self.wc_dn = P.dram("wc_dn", [128, FC, D], BF16)
        self.hT_d = P.dram("hT_d", [D, T + 2], BF16)
        self.oT_d = P.dram("oT_d", [D, T], BF16)
        self.ones = P.sb("ones", [128, 128], BF16)
        P.op("dve", lambda e: e.memset(self.ones.t[:], 1.0), writes=[self.ones])

        self.spt = P.sb("spt", [128, NSP], F32)
        self.cm = P.sb("cm", [128, 4, 128], BF16)
        self.ps = []
        for i in range(8):
            t = self.es.enter_context(nc.psum_tensor("ps%d" % i, [128, 512], F32))
            self.ps.append(TL(t, self.S.buf("ps%d" % i)))
        self.epsc = P.sb("epsc", [128, 1], F32)
        P.op("dve", lambda e: e.memset(self.epsc.t[:], EPS), writes=[self.epsc])
        P.dma("sp", self.spt.t[:], self.spd.t, self.spt)
        P.dma("sp", self.cm.t[:], self.cmat.t, self.cm)
        z = P.sb("zero", [128, KC, 2], BF16)
        P.op("dve", lambda e: e.memset(z.t[:], 0.0), writes=[z])
        hv = self.hT_d.t.rearrange("(k p) t -> p k t", p=128)
        P.dma("sp", hv[:, :, 0:1], z.t[:, :, 0:1], self.hT_d, reads=[z], allow_slow_non_contiguous=True)
        P.dma("sp", hv[:, :, T + 1:T + 2], z.t[:, :, 1:2], self.hT_d, reads=[z], allow_slow_non_contiguous=True)

        chain = []
        subl = [(l, p) for l in self.layers for p in ("mix", "ffn") if p in self.parts]
        n = len(subl)
        bufs = [self.xs, self.y]
        cur = self.x_in
        for i, (l, p) in enumerate(subl):
            dst = bufs[(n - i) % 2]
            chain.append((l, p, cur, dst))
            cur = dst
        for (l, p, src, dst) in chain:
            if p == "ffn":
                self.ffn(l, src, dst)
            elif l % 2 == 0:
                self.hgrn(l, src, dst)
            else:
                self.attn(l, src, dst)
        self.S.emit(self.es, final_bufs=[self.y.b])
        self.es.close()
        return nc

    def spcol(self, c, n=1):
        return self.spt.t[:, c:c + n]

    def load_gain(self, idx, tl):
        self.dma("sp", tl.t[:], self.gains.t[idx], tl)

    def rstd(self, ss, rs, n, parts=128):
        self.op("act", lambda e: e.activation(out=rs.t[:parts], in_=ss.t[:parts], func=AF.Sqrt, scale=1.0 / n, bias=self.epsc.t[:parts, 0:1]),
                reads=[ss, self.epsc], writes=[rs])
        self.op("dve", lambda e: e.reciprocal(out=rs.t[:parts], in_=rs.t[:parts]), reads=[rs], writes=[rs])

    def norm_phase(self, src, gidx):
        P, nc, T = self, self.nc, self.T
        with P.phase():
            g = P.sb("ng", [128, D], F32)
            P.load_gain(gidx, g)
            xt = [P.sb("nx%d" % i, [128, D], F32) for i in range(2)]
            junk = P.sb("njunk", [128, D], BF16)
            hb = [P.sb("nhb%d" % i, [128, D], BF16) for i in range(2)]
            ss = [P.sb("nss%d" % i, [128, 1], F32) for i in range(2)]
            rs = [P.sb("nrs%d" % i, [128, 1], F32) for i in range(2)]
            hTt = [P.sb("nhT%d" % i, [128, KC, 128], BF16) for i in range(2)]
            hv = self.hT_d.t.rearrange("(k p) t -> p k t", p=128)
            P.dma("sp", xt[0].t[:], src.t[0:128, :], xt[0], reads=[src])
            for i in range(self.NTT):
                s = i % 2
                if i + 1 < self.NTT:
                    P.dma("sp", xt[1 - s].t[:], src.t[(i + 1) * 128:(i + 2) * 128, :], xt[1 - s], reads=[src])
                P.op("act", lambda e, s=s: e.activation(out=junk.t[:], in_=xt[s].t[:], func=AF.Square, accum_out=ss[s].t[:]),
                     reads=[xt[s]], writes=[junk, ss[s]])
                P.rstd(ss[s], rs[s], D)
                P.op("dve", lambda e, s=s: e.scalar_tensor_tensor(out=hb[s].t[:], in0=xt[s].t[:], scalar=rs[s].t[:, 0:1], in1=g.t[:],
                                                                  op0=ALU.mult, op1=ALU.mult), reads=[xt[s], rs[s], g], writes=[hb[s]])
                for half in range(2):
                    pb = self.ps[half]
                    pv = pb.t.bitcast(BF16)
                    for kk in range(8):
                        k = half * 8 + kk
                        P.op("pe", lambda e, s=s, k=k, kk=kk, pv=pv: e.transpose(pv[:, kk * 128:(kk + 1) * 128], hb[s].t[:, k * 128:(k + 1) * 128], self.cm.t[:, 0, :]),
                             reads=[hb[s], self.cm], writes=[pb])
                    P.op("act" if half == 0 else "dve",
                         (lambda e, s=s, half=half, pv=pv: e.activation(out=hTt[s].t[:, half * 8:(half + 1) * 8, :], in_=pv[:, :].rearrange("p (k t) -> p k t", k=8), func=AF.Copy))
                         if half == 0 else
                         (lambda e, s=s, half=half, pv=pv: e.tensor_copy(out=hTt[s].t[:, half * 8:(half + 1) * 8, :], in_=pv[:, :].rearrange("p (k t) -> p k t", k=8))),
                         reads=[pb], writes=[hTt[s]])
                P.dma("sp", hv[:, :, 1 + i * 128:1 + (i + 1) * 128], hTt[s].t[:], self.hT_d, reads=[hTt[s]])

    def post_tile(self, m_ap, m_tl, gpost, src, dst, tt, tiles):
        P = self
        xt, junk, ss, rs, tmp = tiles["xt"], tiles["junk"], tiles["ss"], tiles["rs"], tiles["tmp"]
        P.dma("sp", xt.t[:], src.t[tt * 128:(tt + 1) * 128, :], xt, reads=[src])
        P.op("act", lambda e: e.activation(out=junk.t[:], in_=m_ap, func=AF.Square, accum_out=ss.t[:]),
             reads=[m_tl], writes=[junk, ss])
        P.rstd(ss, rs, D)
        P.op("dve", lambda e: e.scalar_tensor_tensor(out=tmp.t[:], in0=m_ap, scalar=rs.t[:, 0:1], in1=gpost.t[:],
                                                     op0=ALU.mult, op1=ALU.mult), reads=[m_tl, rs, gpost], writes=[tmp])
        P.op("dve", lambda e: e.tensor_tensor(out=tmp.t[:], in0=tmp.t[:], in1=xt.t[:], op=ALU.add), reads=[tmp, xt], writes=[tmp])
        P.dma("sp", dst.t[tt * 128:(tt + 1) * 128, :], tmp.t[:], dst, reads=[tmp])

    def post_tiles_alloc(self, i):
        P = self
        return {"xt": P.sb("pxt%d" % i, [128, D], F32), "junk": P.sb("pjunk%d" % i, [128, D], BF16),
                "ss": P.sb("pss%d" % i, [128, 1], F32), "rs": P.sb("prs%d" % i, [128, 1], F32),
                "tmp": P.sb("ptmp%d" % i, [128, D], F32)}

    def ffn(self, l, src, dst):
        P, nc, T, MT = self, self.nc, self.T, self.MT
        P.norm_phase(src, 2 * 4 + l)
        TF = 512
        NG = T // TF
        with P.phase():
            gpost = P.sb("fg", [128, D], F32)
            P.load_gain(3 * 4 + l, gpost)
            hTw = [P.sb("fhT%d" % i, [128, KC, TF + 2], BF16) for i in range(2)]
            NWS = 3
            wup = [P.sb("fwu%d" % i, [128, KC, 128], BF16) for i in range(NWS)]
            wupA = [P.sb("fwuA%d" % i, [128, KC, 128], BF16) for i in range(2)]
            U = [P.sb("fU%d" % i, [128, TF + 2], F32) for i in range(2)]
            acc = [P.sb("facc%d" % i, [128, TF], F32) for i in range(2)]
            gl = P.sb("fgl", [128, TF], F32)
            gT = P.sb("fgT", [128, FC, TF], BF16)
            NWD = 3
            wd = [P.sb("fwd%d" % i, [128, 4, 512], BF16) for i in range(NWD)]
            wdA = [P.sb("fwdA%d" % i, [128, 4, 512], BF16) for i in range(2)]
            msb = [P.sb("fm%d" % i, [128, D], F32) for i in range(4)]
            pt0 = P.post_tiles_alloc(0)
            pt = [pt0, pt0]
            hv = self.hT_d.t.rearrange("(k p) t -> p k t", p=128)
            wi = 0
            wdi = 0
            flag = P.spcol(SP_FLAG)
            for tg in range(NG):
                hw = hTw[tg % 2]
                P.dma("sp", hw.t[:], hv[:, :, tg * TF:tg * TF + TF + 2], hw, reads=[self.hT_d])
                if tg * TF == MT:
                    P.op("dve", lambda e, hw=hw: e.tensor_scalar(out=hw.t[:, :, 0:1], in0=hw.t[:, :, 0:1], scalar1=flag, scalar2=None, op0=ALU.mult),
                         reads=[hw, self.spt], writes=[hw])
                if (tg + 1) * TF == MT:
                    P.op("dve", lambda e, hw=hw: e.tensor_scalar(out=hw.t[:, :, TF + 1:TF + 2], in0=hw.t[:, :, TF + 1:TF + 2], scalar1=flag, scalar2=None, op0=ALU.mult),
                         reads=[hw, self.spt], writes=[hw])
                for j in range(FC):
                    for part in range(2):
                        mch = part * FC + j
                        if tg == 0:
                            w = wupA[wi % 2]; wi += 1
                            P.dma("pool", w.t[:], self.ffn_w_up.t[l, mch], w)
                            P.dma("sp", self.wc_up.t[mch], w.t[:], self.wc_up, reads=[w])
                        else:
                            w = wup[wi % NWS]; wi += 1
                            P.dma("sp", w.t[:], self.wc_up.t[mch], w, reads=[self.wc_up])
                        pm = self.ps[(2 * j + part) % 2]
                        ph = self.ps[2]
                        hcol = ((2 * j + part) % 2) * 2
                        for k in range(KC):
                            P.op("pe", lambda e, w=w, hw=hw, k=k, pm=pm: e.matmul(pm.t[:, :], lhsT=w.t[:, k, :], rhs=hw.t[:, k, 1:TF + 1], start=(k == 0), stop=(k == KC - 1)),
                                 reads=[w, hw], writes=[pm])
                        for k in range(KC):
                            P.op("pe", lambda e, w=w, hw=hw, k=k, ph=ph, hcol=hcol: e.matmul(ph.t[:, hcol:hcol + 2], lhsT=w.t[:, k, :], rhs=hw.t[:, k, 0:TF + 2:TF + 1], start=(k == 0), stop=(k == KC - 1)),
                                 reads=[w, hw], writes=[ph])
                        u = U[part]
                        P.op("act", lambda e, u=u, pm=pm: e.activation(out=u.t[:, 1:TF + 1], in_=pm.t[:, :], func=AF.Copy), reads=[pm], writes=[u])
                        P.op("dve", lambda e, u=u, ph=ph, hcol=hcol: e.tensor_copy(out=u.t[:, 0:TF + 2:TF + 1], in_=ph.t[:, hcol:hcol + 2]), reads=[ph], writes=[u])
                        a = acc[part]
                        cw = lambda jj, mch=mch: P.spcol(SP_CONVW + (l * 3 + jj) * 88 + mch)
                        P.op("dve", lambda e, u=u, a=a, cw=cw: e.tensor_scalar(out=a.t[:], in0=u.t[:, 0:TF], scalar1=cw(0), scalar2=None, op0=ALU.mult),
                             reads=[u, self.spt], writes=[a])
                        P.op("dve", lambda e, u=u, a=a, cw=cw: e.scalar_tensor_tensor(out=a.t[:], in0=u.t[:, 1:TF + 1], scalar=cw(1), in1=a.t[:], op0=ALU.mult, op1=ALU.add),
                             reads=[u, a, self.spt], writes=[a])
                        P.op("dve", lambda e, u=u, a=a, cw=cw: e.scalar_tensor_tensor(out=a.t[:], in0=u.t[:, 2:TF + 2], scalar=cw(2), in1=a.t[:], op0=ALU.mult, op1=ALU.add),
                             reads=[u, a, self.spt], writes=[a])
                    bg = P.spcol(SP_CONVB + l * 88 + j)
                    bv = P.spcol(SP_CONVB + l * 88 + FC + j)
                    P.gelu(gl, acc[0], bg)
                    P.op("dve", lambda e, j=j, bv=bv: e.scalar_tensor_tensor(out=gT.t[:, j, :], in0=acc[1].t[:], scalar=bv, in1=gl.t[:], op0=ALU.add, op1=ALU.mult),
                         reads=[acc[1], gl, self.spt], writes=[gT])
                for cb in range(4):
                    for kq in range(FC // 4):
                        if tg == 0:
                            w = wdA[wdi % 2]; wdi += 1
                            P.dma("pool", w.t[:], self.ffn_w_down.t[l, :, kq * 4:(kq + 1) * 4, cb * 512:(cb + 1) * 512], w)
                            P.dma("sp", self.wc_dn.t[:, kq * 4:(kq + 1) * 4, cb * 512:(cb + 1) * 512], w.t[:], self.wc_dn, reads=[w])
                        else:
                            w = wd[wdi % NWD]; wdi += 1
                            P.dma("sp", w.t[:], self.wc_dn.t[:, kq * 4:(kq + 1) * 4, cb * 512:(cb + 1) * 512], w, reads=[self.wc_dn])
                        for kk in range(4):
                            k = kq * 4 + kk
                            for sub in range(4):
                                pd = self.ps[4 + sub]
                                P.op("pe", lambda e, w=w, kk=kk, k=k, sub=sub, pd=pd: e.matmul(pd.t[:, :], lhsT=gT.t[:, k, sub * 128:(sub + 1) * 128], rhs=w.t[:, kk, :], start=(k == 0), stop=(k == FC - 1)),
                                     reads=[w, gT], writes=[pd])
                    for sub in range(4):
                        pd = self.ps[4 + sub]
                        P.op("act", lambda e, sub=sub, cb=cb, pd=pd: e.activation(out=msb[sub].t[:, cb * 512:(cb + 1) * 512], in_=pd.t[:, :], func=AF.Copy),
                             reads=[pd], writes=[msb[sub]])
                for sub in range(4):
                    P.post_tile(msb[sub].t[:], msb[sub], gpost, src, dst, tg * 4 + sub, pt[sub % 2])

    def gelu(self, out, a, bias):
        P = self
        if getattr(self, "native_gelu", True):
            P.op("act", lambda e: e.activation(out=out.t[:], in_=a.t[:], func=AF.Gelu_apprx_tanh, bias=bias), reads=[a, self.spt], writes=[out])
        else:
            P.op("act", lambda e: e.activation(out=a.t[:], in_=a.t[:], func=AF.Identity, bias=bias), reads=[a, self.spt], writes=[a])
            P.op("dve", lambda e: e.tensor_tensor(out=out.t[:], in0=a.t[:], in1=a.t[:], op=ALU.mult), reads=[a], writes=[out])
            P.op("dve", lambda e: e.tensor_scalar(out=out.t[:], in0=out.t[:], scalar1=0.044715, scalar2=1.0, op0=ALU.mult, op1=ALU.add), reads=[out], writes=[out])
            P.op("dve", lambda e: e.tensor_tensor(out=out.t[:], in0=out.t[:], in1=a.t[:], op=ALU.mult), reads=[out, a], writes=[out])
            P.op("act", lambda e: e.activation(out=out.t[:], in_=out.t[:], func=AF.Sigmoid, scale=1.5957691216057308), reads=[out], writes=[out])
            P.op("dve", lambda e: e.tensor_tensor(out=out.t[:], in0=out.t[:], in1=a.t[:], op=ALU.mult), reads=[out, a], writes=[out])

    def out_proj(self, w_dram, gidx, src, dst):
        P, T = self, self.T
        with P.phase():
            gpost = P.sb("og", [128, D], F32)
            P.load_gain(gidx, gpost)
            W = P.sb("oW", [128, KC, D], BF16)
            for q in range(KC):
                P.dma("pool", W.t[:, q, :], w_dram[:, q, :], W)
            og = [P.sb("oo%d" % i, [128, KC, 512], BF16) for i in range(2)]
            msb = [P.sb("om%d" % i, [128, D], F32) for i in range(2)]
            pt = [P.post_tiles_alloc(i) for i in range(2)]
            ov = self.oT_d.t.rearrange("(k p) t -> p k t", p=128)
            for tg in range(T // 512):
                o = og[tg % 2]
                P.dma("sp", o.t[:], ov[:, :, tg * 512:(tg + 1) * 512], o, reads=[self.oT_d])
                for sub in range(4):
                    m = msb[sub % 2]
                    for cb in range(4):
                        pd = self.ps[4 + cb]
                        for k in range(KC):
                            P.op("pe", lambda e, o=o, k=k, sub=sub, cb=cb, pd=pd: e.matmul(pd.t[:, :], lhsT=o.t[:, k, sub * 128:(sub + 1) * 128], rhs=W.t[:, k, cb * 512:(cb + 1) * 512], start=(k == 0), stop=(k == KC - 1)),
                                 reads=[o, W], writes=[pd])
                        P.op("act", lambda e, m=m, cb=cb, pd=pd: e.activation(out=m.t[:, cb * 512:(cb + 1) * 512], in_=pd.t[:, :], func=AF.Copy), reads=[pd], writes=[m])
                    P.post_tile(m.t[:], m, gpost, src, dst, tg * 4 + sub, pt[sub % 2])


    def hgrn(self, l, src, dst):
        P, nc, T, MT = self, self.nc, self.T, self.MT
        j = l // 2
        NCH = MT // CH
        P.norm_phase(src, 0 * 4 + l)
        TP = min(T, 2048)
        hv = self.hT_d.t.rearrange("(k p) t -> p k t", p=128)
        if not hasattr(self, "lbv"):
            self.lbv = P.sb("lbv", [128, 16], F32); self.omlb = P.sb("omlb", [128, 16], F32)
            self.lbE = P.sb("lbE", [128, 64], F32); self.lbn = P.sb("lbn", [128, 16], F32); self.lbd = P.sb("lbd", [128, 16], F32)
        lbv, omlb, E4, lbn, lbd = self.lbv, self.omlb, self.lbE, self.lbn, self.lbd
        if l == 0:
            P.op("dve", lambda e: e.memset(lbv.t[:], 0.0), writes=[lbv])
        else:
            P.op("act", lambda e: e.activation(out=E4.t[:, :], in_=self.spt.t[:, SP_LB:SP_LB + 64], func=AF.Exp), reads=[self.spt], writes=[E4])
            P.op("dve", lambda e: e.tensor_tensor(out=lbd.t[:, :], in0=E4.t[:, 0:16], in1=E4.t[:, 16:32], op=ALU.add), reads=[E4], writes=[lbd])
            P.op("dve", lambda e: e.tensor_tensor(out=lbd.t[:, :], in0=lbd.t[:, :], in1=E4.t[:, 32:48], op=ALU.add), reads=[E4, lbd], writes=[lbd])
            P.op("dve", lambda e: e.tensor_tensor(out=lbd.t[:, :], in0=lbd.t[:, :], in1=E4.t[:, 48:64], op=ALU.add), reads=[E4, lbd], writes=[lbd])
            P.op("dve", lambda e: e.tensor_copy(out=lbn.t[:, :], in_=E4.t[:, 16:32]), reads=[E4], writes=[lbn])
            for i in range(2, l + 1):
                P.op("dve", lambda e, i=i: e.tensor_tensor(out=lbn.t[:, :], in0=lbn.t[:, :], in1=E4.t[:, i * 16:(i + 1) * 16], op=ALU.add), reads=[E4, lbn], writes=[lbn])
            P.op("dve", lambda e: e.reciprocal(out=lbd.t[:, :], in_=lbd.t[:, :]), reads=[lbd], writes=[lbd])
            P.op("dve", lambda e: e.tensor_tensor(out=lbv.t[:, :], in0=lbn.t[:, :], in1=lbd.t[:, :], op=ALU.mult), reads=[lbn, lbd], writes=[lbv])
        P.op("dve", lambda e: e.tensor_scalar(out=omlb.t[:, :], in0=lbv.t[:, :], scalar1=-1.0, scalar2=1.0, op0=ALU.mult, op1=ALU.add), reads=[lbv], writes=[omlb])
        with P.phase():
            hT = P.sb("hhT", [128, KC, TP], BF16)
            wq = [P.sb("hwq%d" % i, [128, KC, 128], BF16) for i in range(3)]
            sg = [P.sb("hsg%d" % i, [128, 512], F32) for i in range(2)]
            wv_ = [P.sb("hwv%d" % i, [128, 512], F32) for i in range(2)]
            lft = [P.sb("hlf%d" % i, [128, 512], F32) for i in range(2)]
            kk = [P.sb("hkk%d" % i, [128, 512], BF16) for i in range(2)]
            ob = [P.sb("hob%d" % i, [128, 512], BF16) for i in range(3)]
            wi = 0; it = 0; oi = 0; gi = 0
            for p in range(T // TP):
                P.dma("sp", hT.t[:], hv[:, :, 1 + p * TP:1 + (p + 1) * TP], hT, reads=[self.hT_d])
                for h in range(NH):
                    for qty in range(5):
                        mch = qty * 16 + h
                        w = wq[wi % 3]; wi += 1
                        P.dma("pool", w.t[:], self.hg_w_in.t[j, mch], w)
                        for tb in range(TP // 512):
                            pm = self.ps[it % 4]; it += 1
                            for k in range(KC):
                                P.op("pe", lambda e, w=w, k=k, tb=tb, pm=pm: e.matmul(pm.t[:, :], lhsT=w.t[:, k, :], rhs=hT.t[:, k, tb * 512:(tb + 1) * 512], start=(k == 0), stop=(k == KC - 1)),
                                     reads=[w, hT], writes=[pm])
                            r0 = h * 128; c0 = p * TP + tb * 512
                            if qty in (0, 4, 3):
                                o = ob[oi % 3]; oi += 1
                                if qty == 3:
                                    P.op("dve", lambda e, o=o, pm=pm: e.tensor_copy(out=o.t[:, :], in_=pm.t[:, :]), reads=[pm], writes=[o])
                                else:
                                    P.op("act", lambda e, o=o, pm=pm: e.activation(out=o.t[:, :], in_=pm.t[:, :], func=AF.Silu), reads=[pm], writes=[o])
                                dd = {0: self.qs_d, 4: self.gs_d, 3: self.vT_d}[qty]
                                P.dma("sp", dd.t[r0:r0 + 128, c0:c0 + 512], o.t[:, :], dd, reads=[o])
                            else:
                                g_ = gi % 2; gi += 1
                                P.op("act", lambda e, g_=g_, pm=pm: e.activation(out=sg[g_].t[:, :], in_=pm.t[:, :], func=AF.Sigmoid), reads=[pm], writes=[sg[g_]])
                                P.op("dve", lambda e, g_=g_, h=h: e.tensor_scalar(out=wv_[g_].t[:, :], in0=sg[g_].t[:, :], scalar1=1e-30, scalar2=omlb.t[:, h:h + 1], op0=ALU.max, op1=ALU.mult),
                                     reads=[sg[g_], omlb], writes=[wv_[g_]])
                                P.op("act", lambda e, g_=g_, h=h: e.activation(out=lft[g_].t[:, :], in_=wv_[g_].t[:, :], func=AF.Ln, bias=lbv.t[:, h:h + 1]), reads=[wv_[g_], lbv], writes=[lft[g_]])
                                P.op("dve", lambda e, g_=g_, h=h: e.tensor_scalar(out=kk[g_].t[:, :], in0=wv_[g_].t[:, :], scalar1=-1.0, scalar2=omlb.t[:, h:h + 1], op0=ALU.mult, op1=ALU.add),
                                     reads=[wv_[g_], omlb], writes=[kk[g_]])
                                ld = self.lf_d if qty == 1 else self.lb_d
                                kd = self.kf_d if qty == 1 else self.kb_d
                                P.dma("sp", ld.t[r0:r0 + 128, c0:c0 + 512], lft[g_].t[:, :], ld, reads=[lft[g_]])
                                P.dma("sp", kd.t[r0:r0 + 128, c0:c0 + 512], kk[g_].t[:, :], kd, reads=[kk[g_]])
        with P.phase():
            ones32 = P.sb("sone", [128, MT], F32)
            P.op("dve", lambda e: e.memset(ones32.t[:], 1.0), writes=[ones32])
            q_t = P.sb("sq", [128, MT], BF16); v_t = P.sb("sv", [128, MT], BF16)
            k_t = [P.sb("sk%d" % d, [128, MT], BF16) for d in range(2)]
            lf_t = [P.sb("slf%d" % d, [128, MT], F32) for d in range(2)]
            Bp = [P.sb("sBp%d" % d, [128, MT + 1], F32) for d in range(2)]
            D1 = P.sb("sD1", [128, MT], F32); E = P.sb("sE", [128, MT], F32)
            qt = [P.sb("sqt%d" % d, [128, MT], BF16) for d in range(2)]
            kt = [P.sb("skt%d" % d, [128, MT], BF16) for d in range(2)]
            kh = [P.sb("skh%d" % d, [128, MT], BF16) for d in range(2)]
            qc = P.sb("sqc", [128, MT], BF16)
            r = [P.sb("sr%d" % d, [128, NCH], F32) for d in range(2)]
            rn = [P.sb("srn%d" % d, [128, NCH], F32) for d in range(2)]
            g = [P.sb("sg%d" % d, [128, NCH], F32) for d in range(2)]
            nb = P.sb("snb", [128, 1], F32)
            v_tok = P.sb("svt", [64, NCH, 128], BF16)
            kh_tok = [P.sb("skt_%d" % d, [64, NCH, 128], BF16) for d in range(2)]
            M32 = [[P.sb("sM%d%d" % (d, i), [128, 128], F32) for i in range(2)] for d in range(2)]
            Mb = [[P.sb("sMb%d%d" % (d, i), [128, 128], BF16) for i in range(2)] for d in range(2)]
            ATm = [P.sb("sAT%d" % i, [64, 64], BF16) for i in range(4)]
            o32 = P.sb("so32", [128, MT], F32)
            for d in range(2):
                P.op("dve", lambda e, d=d: e.memset(Bp[d].t[:, 0:1], 0.0), writes=[Bp[d]])
            pA = self.ps[0]; pU = self.ps[1]
            ai = 0; ui = 0; ti = 0
            for h in range(NH):
                r0 = h * 128
                for seg in range(2):
                    c0 = seg * MT
                    P.dma("sp", q_t.t[:], self.qs_d.t[r0:r0 + 128, c0:c0 + MT], q_t, reads=[self.qs_d])
                    P.dma("sp", v_t.t[:], self.vT_d.t[r0:r0 + 128, c0:c0 + MT], v_t, reads=[self.vT_d])
                    for d in range(2):
                        kd = self.kf_d if d == 0 else self.kb_d
                        ld = self.lf_d if d == 0 else self.lb_d
                        P.dma("sp", k_t[d].t[:], kd.t[r0:r0 + 128, c0:c0 + MT], k_t[d], reads=[kd])
                        P.dma("sp", lf_t[d].t[:], ld.t[r0:r0 + 128, c0:c0 + MT], lf_t[d], reads=[ld])
                    def tok_T(srct, dstt):
                        nonlocal ti
                        for c in range(NCH):
                            pb = self.ps[2 + (ti // 8) % 2]
                            pv = pb.t.bitcast(BF16)
                            P.op("pe", lambda e, c=c, pv=pv, sl=ti % 8: e.transpose(pv[0:64, sl * 128:(sl + 1) * 128], srct.t[:, c * 64:(c + 1) * 64], self.cm.t[:, 0, :]),
                                 reads=[srct, self.cm], writes=[pb])
                            ti += 1
                            if ti % 8 == 0:
                                cs_ = c - 7
                                P.op("act", lambda e, pv=pv, cs_=cs_: e.activation(out=dstt.t[:, cs_:cs_ + 8, :], in_=pv[0:64, :].rearrange("p (c f) -> p c f", f=128), func=AF.Copy),
                                     reads=[pb], writes=[dstt])
                    tok_T(v_t, v_tok)
                    for d in range(2):
                        off = 1 if d == 0 else 0
                        Bt = Bp[d].t[:, off:off + MT]
                        Btv = Bt.rearrange("p (c j) -> p c j", j=CH)
                        D1v = D1.t[:, :].rearrange("p (c j) -> p c j", j=CH)
                        P.op("dve", lambda e, d=d: e.tensor_tensor_scan(out=Bp[d].t[:, 1:MT + 1], data0=ones32.t[:, :], data1=lf_t[d].t[:, :], initial=0.0,
                                                                        op0=ALU.mult, op1=(ALU.add if d == 0 else ALU.subtract)),
                             reads=[ones32, lf_t[d]], writes=[Bp[d]])
                        P.op("dve", lambda e, d=d, Btv=Btv: e.tensor_copy(out=r[d].t[:, :], in_=Btv[:, :, CH // 2]), reads=[Bp[d]], writes=[r[d]])
                        if d == 0:
                            P.op("dve", lambda e, d=d: e.tensor_copy(out=rn[d].t[:, 0:NCH - 1], in_=r[d].t[:, 1:NCH]), reads=[r[d]], writes=[rn[d]])
                            P.op("dve", lambda e, d=d: e.tensor_copy(out=rn[d].t[:, NCH - 1:NCH], in_=Bp[d].t[:, MT:MT + 1]), reads=[Bp[d], rn[d]], writes=[rn[d]])
                        else:
                            P.op("dve", lambda e, d=d: e.tensor_copy(out=rn[d].t[:, 1:NCH], in_=r[d].t[:, 0:NCH - 1]), reads=[r[d]], writes=[rn[d]])
                            P.op("dve", lambda e, d=d: e.tensor_copy(out=rn[d].t[:, 0:1], in_=Bp[d].t[:, 0:1]), reads=[Bp[d], rn[d]], writes=[rn[d]])
                        P.op("dve", lambda e, d=d, Btv=Btv, D1v=D1v: e.tensor_tensor(out=D1v, in0=Btv, in1=bcast(r[d].t[:, :], CH), op=ALU.subtract), reads=[Bp[d], r[d]], writes=[D1])
                        P.op("act", lambda e: e.activation(out=E.t[:, :], in_=D1.t[:, :], func=AF.Exp), reads=[D1], writes=[E])
                        P.op("dve", lambda e, d=d: e.tensor_tensor(out=qt[d].t[:, :], in0=q_t.t[:, :], in1=E.t[:, :], op=ALU.mult), reads=[q_t, E], writes=[qt[d]])
                        P.op("act", lambda e: e.activation(out=E.t[:, :], in_=D1.t[:, :], func=AF.Exp, scale=-1.0), reads=[D1], writes=[E])
                        P.op("dve", lambda e, d=d: e.tensor_tensor(out=kt[d].t[:, :], in0=k_t[d].t[:, :], in1=E.t[:, :], op=ALU.mult), reads=[k_t[d], E], writes=[kt[d]])
                        P.op("dve", lambda e, d=d, Btv=Btv, D1v=D1v: e.tensor_tensor(out=D1v, in0=Btv, in1=bcast(rn[d].t[:, :], CH), op=ALU.subtract), reads=[Bp[d], rn[d]], writes=[D1])
                        P.op("act", lambda e: e.activation(out=E.t[:, :], in_=D1.t[:, :], func=AF.Exp, scale=-1.0), reads=[D1], writes=[E])
                        P.op("dve", lambda e, d=d: e.tensor_tensor(out=kh[d].t[:, :], in0=k_t[d].t[:, :], in1=E.t[:, :], op=ALU.mult), reads=[k_t[d], E], writes=[kh[d]])
                        P.op("dve", lambda e, d=d: e.tensor_tensor(out=g[d].t[:, :], in0=rn[d].t[:, :], in1=r[d].t[:, :], op=ALU.subtract), reads=[rn[d], r[d]], writes=[g[d]])
                        P.op("act", lambda e, d=d: e.activation(out=g[d].t[:, :], in_=g[d].t[:, :], func=AF.Exp), reads=[g[d]], writes=[g[d]])
                        if (seg == 1 and d == 0) or (seg == 0 and d == 1):
                            if d == 0:
                                P.op("act", lambda e, Bt=Bt: e.activation(out=E.t[:, :], in_=Bt, func=AF.Exp), reads=[Bp[d]], writes=[E])
                            else:
                                P.op("dve", lambda e, d=d: e.tensor_scalar(out=nb.t[:, :], in0=Bp[d].t[:, MT:MT + 1], scalar1=-1.0, scalar2=None, op0=ALU.mult), reads=[Bp[d]], writes=[nb])
                                P.op("act", lambda e, Bt=Bt: e.activation(out=E.t[:, :], in_=Bt, func=AF.Exp, bias=nb.t[:, 0:1]), reads=[Bp[d], nb], writes=[E])
                            P.op("dve", lambda e: e.tensor_tensor(out=qc.t[:, :], in0=q_t.t[:, :], in1=E.t[:, :], op=ALU.mult), reads=[q_t, E], writes=[qc])
                            qd = self.qcf_d if d == 0 else self.qcb_d
                            P.dma("sp", qd.t[r0:r0 + 128, c0:c0 + MT], qc.t[:, :], qd, reads=[qc])
                        tok_T(kh[d], kh_tok[d])
                    units = [(i, d) for i in range(NCH) for d in range(2)]
                    ams = {}

                    def emit_AT(u):
                        nonlocal ai
                        i, d = units[u]
                        c = i if d == 0 else NCH - 1 - i
                        sl = ai % 8; am = ATm[ai % 4]; ai += 1
                        ams[u] = am
                        P.op("pe", lambda e, d=d, c=c, sl=sl: e.matmul(pA.t[0:64, sl * 64:(sl + 1) * 64], lhsT=kt[d].t[:, c * 64:(c + 1) * 64], rhs=qt[d].t[:, c * 64:(c + 1) * 64], start=True, stop=True),
                             reads=[kt[d], qt[d]], writes=[pA])
                        P.op("dve", lambda e, d=d, sl=sl, am=am: e.tensor_tensor(out=am.t[:, :], in0=pA.t[0:64, sl * 64:(sl + 1) * 64], in1=self.cm.t[0:64, 2 + d, 0:64], op=ALU.mult),
                             reads=[pA, self.cm], writes=[am])

                    emit_AT(0)
                    for u, (i, d) in enumerate(units):
                        if u + 1 < len(units):
                            emit_AT(u + 1)
                        c = i if d == 0 else NCH - 1 - i
                        fw = 0 if c <= NCH - 1 - c else 1
                        cur = i % 2; nxt = 1 - cur
                        am = ams.pop(u)
                        ul = ui % 4; ui += 1
                        P.op("pe", lambda e, d=d, c=c, ul=ul: e.matmul(pU.t[:, ul * 128:(ul + 1) * 128], lhsT=kh_tok[d].t[:, c, :], rhs=v_tok.t[:, c, :], start=True, stop=True),
                             reads=[kh_tok[d], v_tok], writes=[pU])
                        po = self.ps[4 + u % 4]
                        P.op("pe", lambda e, d=d, c=c, am=am, po=po, i=i: e.matmul(po.t[:, 0:64], lhsT=v_tok.t[:, c, :], rhs=am.t[:, :], start=True, stop=(i == 0)),
                             reads=[v_tok, am], writes=[po])
                        if i > 0:
                            P.op("pe", lambda e, d=d, c=c, po=po, cur=cur: e.matmul(po.t[:, 0:64], lhsT=Mb[d][cur].t[:, :], rhs=qt[d].t[:, c * 64:(c + 1) * 64], start=False, stop=True),
                                 reads=[Mb[d][cur], qt[d]], writes=[po])
                        if i == 0:
                            P.op("dve", lambda e, d=d, ul=ul, nxt=nxt: e.tensor_copy(out=M32[d][nxt].t[:, :], in_=pU.t[:, ul * 128:(ul + 1) * 128]), reads=[pU], writes=[M32[d][nxt]])
                        else:
                            P.op("dve", lambda e, d=d, c=c, ul=ul, cur=cur, nxt=nxt: e.scalar_tensor_tensor(out=M32[d][nxt].t[:, :], in0=M32[d][cur].t[:, :], scalar=g[d].t[:, c:c + 1], in1=pU.t[:, ul * 128:(ul + 1) * 128], op0=ALU.mult, op1=ALU.add),
                                 reads=[M32[d][cur], g[d], pU], writes=[M32[d][nxt]])
                        P.op("act", lambda e, d=d, nxt=nxt: e.activation(out=Mb[d][nxt].t[:, :], in_=M32[d][nxt].t[:, :], func=AF.Copy), reads=[M32[d][nxt]], writes=[Mb[d][nxt]])
                        if d == fw:
                            P.op("act", lambda e, c=c, po=po: e.activation(out=o32.t[:, c * 64:(c + 1) * 64], in_=po.t[:, 0:64], func=AF.Copy), reads=[po], writes=[o32])
                        else:
                            P.op("dve", lambda e, c=c, po=po: e.tensor_tensor(out=o32.t[:, c * 64:(c + 1) * 64], in0=o32.t[:, c * 64:(c + 1) * 64], in1=po.t[:, 0:64], op=ALU.add), reads=[po, o32], writes=[o32])
                    fin = NCH % 2
                    for d in range(2):
                        P.dma("sp", self.S_d.t[h, seg, d], M32[d][fin].t[:, :], self.S_d, reads=[M32[d][fin]])
                    P.dma("sp", self.o_d.t[r0:r0 + 128, c0:c0 + MT], o32.t[:, :], self.o_d, reads=[o32])
        with P.phase():
            fo32 = [P.sb("fo%d" % i, [128, MT], F32) for i in range(2)]
            gt = [P.sb("fg%d" % i, [128, MT], BF16) for i in range(2)]
            fqc = [P.sb("fq%d" % i, [128, MT], BF16) for i in range(2)]
            Sr = [P.sb("fS%d" % i, [128, 128], F32) for i in range(2)]
            Sb = [P.sb("fSb%d" % i, [128, 128], BF16) for i in range(2)]
            fsq = P.sb("fsq", [128, MT], BF16)
            frst = P.sb("frs", [128, MT], F32)
            og = [P.sb("fog%d" % i, [128, MT], BF16) for i in range(2)]
            flag = P.spcol(SP_FLAG)
            it = 0
            for h in range(NH):
                r0 = h * 128
                for seg in range(2):
                    s_ = it % 2; it += 1
                    c0 = seg * MT
                    P.dma("sp", fo32[s_].t[:], self.o_d.t[r0:r0 + 128, c0:c0 + MT], fo32[s_], reads=[self.o_d])
                    P.dma("sp", gt[s_].t[:], self.gs_d.t[r0:r0 + 128, c0:c0 + MT], gt[s_], reads=[self.gs_d])
                    qd = self.qcf_d if seg == 1 else self.qcb_d
                    P.dma("sp", fqc[s_].t[:], qd.t[r0:r0 + 128, c0:c0 + MT], fqc[s_], reads=[qd])
                    P.dma("sp", Sr[s_].t[:], self.S_d.t[h, 1 - seg, 0 if seg == 1 else 1], Sr[s_], reads=[self.S_d])
                    P.op("dve", lambda e, s_=s_: e.tensor_scalar(out=Sb[s_].t[:, :], in0=Sr[s_].t[:, :], scalar1=flag, scalar2=None, op0=ALU.mult), reads=[Sr[s_], self.spt], writes=[Sb[s_]])
                    for tb in range(MT // 512):
                        pc = self.ps[4 + tb]
                        P.op("pe", lambda e, s_=s_, tb=tb, pc=pc: e.matmul(pc.t[:, :], lhsT=Sb[s_].t[:, :], rhs=fqc[s_].t[:, tb * 512:(tb + 1) * 512], start=True, stop=True), reads=[Sb[s_], fqc[s_]], writes=[pc])
                        P.op("dve", lambda e, s_=s_, tb=tb, pc=pc: e.tensor_tensor(out=fo32[s_].t[:, tb * 512:(tb + 1) * 512], in0=fo32[s_].t[:, tb * 512:(tb + 1) * 512], in1=pc.t[:, :], op=ALU.add),
                             reads=[fo32[s_], pc], writes=[fo32[s_]])
                    P.op("act", lambda e, s_=s_: e.activation(out=fsq.t[:, :], in_=fo32[s_].t[:, :], func=AF.Square), reads=[fo32[s_]], writes=[fsq])
                    for tb in range(MT // 512):
                        pq = self.ps[tb % 4]
                        P.op("pe", lambda e, tb=tb, pq=pq: e.matmul(pq.t[:, :], lhsT=self.ones.t[:, :], rhs=fsq.t[:, tb * 512:(tb + 1) * 512], start=True, stop=True), reads=[self.ones, fsq], writes=[pq])
                        P.op("act", lambda e, tb=tb, pq=pq: e.activation(out=frst.t[:, tb * 512:(tb + 1) * 512], in_=pq.t[:, :], func=AF.Sqrt, scale=1.0 / 128.0, bias=self.epsc.t[:, 0:1]), reads=[pq, self.epsc], writes=[frst])
                    P.op("dve", lambda e: e.reciprocal(out=frst.t[:, :], in_=frst.t[:, :]), reads=[frst], writes=[frst])
                    P.op("dve", lambda e, s_=s_: e.tensor_tensor(out=fo32[s_].t[:, :], in0=fo32[s_].t[:, :], in1=frst.t[:, :], op=ALU.mult), reads=[fo32[s_], frst], writes=[fo32[s_]])
                    P.op("dve", lambda e, s_=s_: e.scalar_tensor_tensor(out=og[s_].t[:, :], in0=fo32[s_].t[:, :], scalar=P.spcol(SP_HGN + j), in1=gt[s_].t[:, :], op0=ALU.mult, op1=ALU.mult),
                         reads=[fo32[s_], gt[s_], self.spt], writes=[og[s_]])
                    P.dma("sp", self.oT_d.t[r0:r0 + 128, c0:c0 + MT], og[s_].t[:, :], self.oT_d, reads=[og[s_]])
        P.out_proj(self.hg_w_out.t[j], 1 * 4 + l, src, dst)


    def attn(self, l, src, dst):
        P, nc, T, MT = self, self.nc, self.T, self.MT
        j = l // 2
        lam_init = 0.8 - 0.6 * math.exp(-0.3 * l)
        P.norm_phase(src, 0 * 4 + l)
        TP = min(T, 1024)
        NK = T // 128
        hv = self.hT_d.t.rearrange("(k p) t -> p k t", p=128)
        if getattr(self, "skip_proj", False):
            return self._attn_core(l, src, dst)
        with P.phase():
            hT = P.sb("ahT", [128, KC, TP], BF16)
            cst = P.sb("acs", [128, 2, TP], F32)
            Wv = P.sb("aWv", [128, KC, D], BF16)
            for q in range(KC):
                P.dma("pool", Wv.t[:, q, :], self.da_w_v.t[j, :, q, :], Wv)
            wq = [P.sb("awq%d" % i, [128, KC, 128], BF16) for i in range(3)]
            xb = [P.sb("axb%d" % i, [128, 512], BF16) for i in range(2)]
            t1 = [P.sb("at1%d" % i, [128, 512], F32) for i in range(2)]
            t2 = [P.sb("at2%d" % i, [128, 512], F32) for i in range(2)]
            ob = [P.sb("aob%d" % i, [128, 512], BF16) for i in range(3)]
            vb = [P.sb("avb%d" % i, [128, D], BF16) for i in range(2)]
            wi = 0; oi = 0; it = 0
            for p in range(T // TP):
                P.dma("sp", hT.t[:], hv[:, :, 1 + p * TP:1 + (p + 1) * TP], hT, reads=[self.hT_d])
                P.dma("sp", cst.t[:], self.cs.t[:, :, p * TP:(p + 1) * TP], cst)
                for mch in range(32):
                    w = wq[wi % 3]; wi += 1
                    P.dma("pool", w.t[:], self.da_w_qk.t[j, mch], w)
                    dstd = self.qT_d if mch < 16 else self.kT_d
                    for tb in range(TP // 512):
                        pm = self.ps[it % 2]; pr = self.ps[2 + it % 2]; sl = it % 2; it += 1
                        for k in range(KC):
                            P.op("pe", lambda e, w=w, k=k, tb=tb, pm=pm: e.matmul(pm.t[:, :], lhsT=w.t[:, k, :], rhs=hT.t[:, k, tb * 512:(tb + 1) * 512], start=(k == 0), stop=(k == KC - 1)),
                                 reads=[w, hT], writes=[pm])
                        o = ob[oi % 3]; oi += 1
                        P.op("act", lambda e, sl=sl, pm=pm: e.activation(out=xb[sl].t[:, :], in_=pm.t[:, :], func=AF.Copy), reads=[pm], writes=[xb[sl]])
                        P.op("pe", lambda e, sl=sl, pr=pr: e.matmul(pr.t[:, :], lhsT=self.cm.t[:, 1, :], rhs=xb[sl].t[:, :], start=True, stop=True),
                             reads=[xb[sl], self.cm], writes=[pr])
                        P.op("act", lambda e, sl=sl, pm=pm: e.activation(out=t1[sl].t[:, :], in_=pm.t[:, :], func=AF.Copy), reads=[pm], writes=[t1[sl]])
                        P.op("dve", lambda e, sl=sl, tb=tb: e.tensor_tensor(out=t1[sl].t[:, :], in0=t1[sl].t[:, :], in1=cst.t[:, 0, tb * 512:(tb + 1) * 512], op=ALU.mult),
                             reads=[t1[sl], cst], writes=[t1[sl]])
                        P.op("dve", lambda e, sl=sl, pr=pr, tb=tb: e.tensor_tensor(out=t2[sl].t[:, :], in0=pr.t[:, :], in1=cst.t[:, 1, tb * 512:(tb + 1) * 512], op=ALU.mult),
                             reads=[pr, cst], writes=[t2[sl]])
                        P.op("dve", lambda e, sl=sl, o=o: e.tensor_tensor(out=o.t[:, :], in0=t1[sl].t[:, :], in1=t2[sl].t[:, :], op=ALU.add),
                             reads=[t1[sl], t2[sl]], writes=[o])
                        r0 = (mch % 16) * 128
                        c0 = p * TP + tb * 512
                        P.dma("sp", dstd.t[r0:r0 + 128, c0:c0 + 512], o.t[:, :], dstd, reads=[o])
                for tt in range(TP // 128):
                    v = vb[tt % 2]
                    for cb in range(4):
                        pd = self.ps[4 + cb]
                        for k in range(KC):
                            P.op("pe", lambda e, k=k, tt=tt, cb=cb, pd=pd: e.matmul(pd.t[:, :], lhsT=hT.t[:, k, tt * 128:(tt + 1) * 128], rhs=Wv.t[:, k, cb * 512:(cb + 1) * 512], start=(k == 0), stop=(k == KC - 1)),
                                 reads=[hT, Wv], writes=[pd])
                        if cb % 2 == 0:
                            P.op("act", lambda e, v=v, cb=cb, pd=pd: e.activation(out=v.t[:, cb * 512:(cb + 1) * 512], in_=pd.t[:, :], func=AF.Copy), reads=[pd], writes=[v])
                        else:
                            P.op("dve", lambda e, v=v, cb=cb, pd=pd: e.tensor_copy(out=v.t[:, cb * 512:(cb + 1) * 512], in_=pd.t[:, :]), reads=[pd], writes=[v])
                    tr = p * TP + tt * 128
                    P.dma("sp", self.v_d.t[tr:tr + 128, :], v.t[:, :], self.v_d, reads=[v])
        return self._attn_core(l, src, dst)

    def _attn_core(self, l, src, dst):
        P, nc, T, MT = self, self.nc, self.T, self.MT
        j = l // 2
        lam_init = 0.8 - 0.6 * math.exp(-0.3 * l)
        NK = T // 128
        if getattr(self, "skip_core", False):
            return P.out_proj(self.da_w_out.t[j], 1 * 4 + l, src, dst)
        with P.phase():
            mb = P.sb("amb", [128, 2, NK], F32)
            P.dma("sp", mb.t[:], self.maskb.t, mb)
            pr_f = P.sb("apr", [128, 2], F32); pr_b = P.sb("aprb", [128, 2], BF16)
            ex = P.sb("aex", [128, 2], F32); nl = P.sb("anl", [128, 1], F32)
            lc = SP_LAM + j * 4
            P.op("dve", lambda e: e.tensor_tensor(out=pr_f.t[:, 0:1], in0=P.spcol(lc + 0), in1=P.spcol(lc + 1), op=ALU.mult), reads=[self.spt], writes=[pr_f])
            P.op("dve", lambda e: e.tensor_tensor(out=pr_f.t[:, 1:2], in0=P.spcol(lc + 2), in1=P.spcol(lc + 3), op=ALU.mult), reads=[self.spt, pr_f], writes=[pr_f])
            P.op("dve", lambda e: e.tensor_copy(out=pr_b.t[:, :], in_=pr_f.t[:, :]), reads=[pr_f], writes=[pr_b])
            P.op("pe", lambda e: e.matmul(self.ps[0].t[:, 0:2], lhsT=self.ones.t[:, :], rhs=pr_b.t[:, :], start=True, stop=True), reads=[self.ones, pr_b], writes=[self.ps[0]])
            P.op("act", lambda e: e.activation(out=ex.t[:, :], in_=self.ps[0].t[:, 0:2], func=AF.Exp), reads=[self.ps[0]], writes=[ex])
            P.op("dve", lambda e: e.tensor_tensor(out=nl.t[:, :], in0=ex.t[:, 1:2], in1=ex.t[:, 0:1], op=ALU.subtract), reads=[ex], writes=[nl])
            P.op("dve", lambda e: e.tensor_scalar(out=nl.t[:, :], in0=nl.t[:, :], scalar1=-lam_init, scalar2=None, op0=ALU.add), reads=[nl], writes=[nl])
            KT = [P.sb("aKT%d" % i, [128, 2, T], BF16) for i in range(2)]
            QT = [P.sb("aQT%d" % i, [128, 2, T], BF16) for i in range(2)]
            Vh = [P.sb("aVh%d" % i, [128, NK, 256], BF16) for i in range(2)]
            PT = [P.sb("aPT%d" % i, [128, 512], BF16) for i in range(3)]
            rz = [P.sb("arz%d" % i, [128, 512], F32) for i in range(2)]
            accD = [P.sb("aaD%d" % i, [128, 512], F32) for i in range(2)]
            accP = [P.sb("aaP%d" % i, [128, 512], F32) for i in range(2)]
            zhi = [P.sb("azh%d" % i, [128, 512], BF16) for i in range(2)]
            zlo = [P.sb("azl%d" % i, [128, 512], BF16) for i in range(2)]
            ta = P.sb("ata", [128, 512], F32); tb_ = P.sb("atb", [128, 512], F32)
            oo = [P.sb("aoo%d" % i, [128, 512], F32) for i in range(2)]
            sq = [P.sb("asq%d" % i, [128, 512], BF16) for i in range(2)]
            rst = P.sb("arst", [128, 512], F32)
            onb = [P.sb("aonb%d" % i, [128, 512], BF16) for i in range(4)]
            vv = self.v_d.t.rearrange("(kt p) c -> p kt c", p=128)
            scale = 128.0 ** -0.5
            pti = 0; oni = 0; sti = 0
            def load_head(h):
                hs = h % 2
                for s_ in range(2):
                    r0 = (2 * h + s_) * 128
                    P.dma("sp", KT[hs].t[:, s_, :], self.kT_d.t[r0:r0 + 128, :], KT[hs], reads=[self.kT_d])
                    P.dma("sp", QT[hs].t[:, s_, :], self.qT_d.t[r0:r0 + 128, :], QT[hs], reads=[self.qT_d])
                P.dma("sp", Vh[hs].t[:], vv[:, :, h * 256:(h + 1) * 256], Vh[hs], reads=[self.v_d])

            load_head(0)
            for h in range(8):
                hs = h % 2
                if h + 1 < 8:
                    load_head(h + 1)
                for qt in range(T // 512):
                    seg = (qt * 512) // MT
                    iters = [(s_, kt) for s_ in range(2) for kt in range(NK)]
                    slots = {}

                    def emit_S(i, hs=hs, qt=qt, seg=seg):
                        nonlocal sti, pti
                        s_, kt = iters[i]
                        pS = self.ps[sti % 2]; sti += 1
                        pt = PT[pti % 3]; pti += 1
                        slots[i] = pt
                        P.op("pe", lambda e, hs=hs, s_=s_, kt=kt, qt=qt, pS=pS: e.matmul(pS.t[:, :], lhsT=KT[hs].t[:, s_, kt * 128:(kt + 1) * 128], rhs=QT[hs].t[:, s_, qt * 512:(qt + 1) * 512], start=True, stop=True),
                             reads=[KT[hs], QT[hs]], writes=[pS])
                        P.op("act", lambda e, pt=pt, pS=pS, seg=seg, kt=kt: e.activation(out=pt.t[:, :], in_=pS.t[:, :], func=AF.Exp, scale=scale, bias=mb.t[:, seg, kt:kt + 1]),
                             reads=[pS, mb], writes=[pt])

                    emit_S(0)
                    for i, (s_, kt) in enumerate(iters):
                        if i + 1 < len(iters):
                            emit_S(i + 1)
                        pt = slots.pop(i)
                        pO = [self.ps[2 + 3 * s_], self.ps[3 + 3 * s_]]; pZ = self.ps[4 + 3 * s_]
                        for half in range(2):
                            P.op("pe", lambda e, hs=hs, kt=kt, half=half, pt=pt, pO=pO: e.matmul(pO[half].t[:, :], lhsT=Vh[hs].t[:, kt, half * 128:(half + 1) * 128], rhs=pt.t[:, :], start=(kt == 0), stop=(kt == NK - 1)),
                                 reads=[Vh[hs], pt], writes=[pO[half]])
                        P.op("pe", lambda e, kt=kt, pt=pt, pZ=pZ: e.matmul(pZ.t[:, :], lhsT=self.ones.t[:, :], rhs=pt.t[:, :], start=(kt == 0), stop=(kt == NK - 1)),
                             reads=[self.ones, pt], writes=[pZ])
                        if kt == NK - 1:
                            P.op("dve", lambda e, s_=s_, pZ=pZ: e.reciprocal(out=rz[s_].t[:, :], in_=pZ.t[:, :]), reads=[pZ], writes=[rz[s_]])
                    pq = self.ps[0]
                    for half in range(2):
                        P.op("dve", lambda e, half=half: e.tensor_tensor(out=ta.t[:, :], in0=self.ps[2 + half].t[:, :], in1=rz[0].t[:, :], op=ALU.mult), reads=[self.ps[2 + half], rz[0]], writes=[ta])
                        P.op("dve", lambda e, half=half: e.tensor_tensor(out=tb_.t[:, :], in0=self.ps[5 + half].t[:, :], in1=rz[1].t[:, :], op=ALU.mult), reads=[self.ps[5 + half], rz[1]], writes=[tb_])
                        P.op("dve", lambda e, half=half: e.scalar_tensor_tensor(out=oo[half].t[:, :], in0=tb_.t[:, :], scalar=nl.t[:, 0:1], in1=ta.t[:, :], op0=ALU.mult, op1=ALU.add),
                             reads=[ta, tb_, nl], writes=[oo[half]])
                        P.op("act", lambda e, half=half: e.activation(out=sq[half].t[:, :], in_=oo[half].t[:, :], func=AF.Square), reads=[oo[half]], writes=[sq[half]])
                        P.op("pe", lambda e, half=half, pq=pq: e.matmul(pq.t[:, :], lhsT=self.ones.t[:, :], rhs=sq[half].t[:, :], start=(half == 0), stop=(half == 1)),
                             reads=[self.ones, sq[half]], writes=[pq])
                    P.op("act", lambda e, pq=pq: e.activation(out=rst.t[:, :], in_=pq.t[:, :], func=AF.Sqrt, scale=1.0 / 256.0, bias=self.epsc.t[:, 0:1]), reads=[pq, self.epsc], writes=[rst])
                    P.op("dve", lambda e: e.reciprocal(out=rst.t[:, :], in_=rst.t[:, :]), reads=[rst], writes=[rst])
                    for half in range(2):
                        on = onb[oni % 4]; oni += 1
                        P.op("dve", lambda e, half=half: e.tensor_tensor(out=oo[half].t[:, :], in0=oo[half].t[:, :], in1=rst.t[:, :], op=ALU.mult), reads=[oo[half], rst], writes=[oo[half]])
                        gcol = P.spcol(SP_SUBLN + j * 2 + half)
                        P.op("dve", lambda e, half=half, on=on, gcol=gcol: e.tensor_scalar(out=on.t[:, :], in0=oo[half].t[:, :], scalar1=gcol, scalar2=(1.0 - lam_init), op0=ALU.mult, op1=ALU.mult),
                             reads=[oo[half], self.spt], writes=[on])
                        r0 = h * 256 + half * 128
                        P.dma("sp", self.oT_d.t[r0:r0 + 128, qt * 512:(qt + 1) * 512], on.t[:, :], self.oT_d, reads=[on])
        if getattr(self, "skip_oproj", False):
            return P.out_proj_dummy(src, dst)
        P.out_proj(self.da_w_out.t[j], 1 * 4 + l, src, dst)

    def out_proj_dummy(self, src, dst):
        P = self
        with P.phase():
            t = P.sb("dmy", [128, D], F32)
            for i in range(self.NTT):
                P.dma("sp", t.t[:], src.t[i * 128:(i + 1) * 128, :], t, reads=[src])
                P.dma("sp", dst.t[i * 128:(i + 1) * 128, :], t.t[:], dst, reads=[t])


def prep_shared(inp, T, need=("hg", "da", "ffn")):
    f = lambda a: np.ascontiguousarray(np.asarray(a, dtype=np.float32))
    out = {}
    g = np.stack([f(inp["pre_mix_g"]), f(inp["post_mix_g"]), f(inp["pre_ffn_g"]), f(inp["post_ffn_g"])], 0).reshape(16, 1, D)
    out["gains"] = np.ascontiguousarray(np.broadcast_to(g, (16, 128, D)))

    def formA(w, ncols):
        return np.ascontiguousarray(w.reshape(KC, 128, ncols // 128, 128).transpose(2, 1, 0, 3))

    def formB(w):
        K, N = w.shape
        return np.ascontiguousarray(w.reshape(K // 128, 128, N).transpose(1, 0, 2))

    if "hg" in need:
        out["hg_w_in"] = np.stack([formA(f(inp["hg_w_in"][j]), 5 * D) for j in range(2)], 0)
        out["hg_w_out"] = np.stack([formB(f(inp["hg_w_out"][j])) for j in range(2)], 0)
    if "da" in need:
        qkv = f(inp["da_w_qkv"])
        out["da_w_qk"] = np.stack([formA(qkv[j][:, :2 * D], 2 * D) for j in range(2)], 0)
        out["da_w_v"] = np.stack([formB(qkv[j][:, 2 * D:]) for j in range(2)], 0)
        out["da_w_out"] = np.stack([formB(f(inp["da_w_out"][j])) for j in range(2)], 0)
    if "ffn" in need:
        out["ffn_w_up"] = np.stack([formA(f(inp["ffn_w_up"][l]), 2 * DFF) for l in range(4)], 0)
        out["ffn_w_down"] = np.stack([formB(f(inp["ffn_w_down"][l])) for l in range(4)], 0)
    sp = np.zeros((128, NSP), np.float32)
    cw = f(inp["ffn_conv_w"])
    sp[:, SP_CONVW:SP_CONVW + 4 * 3 * 88] = cw.reshape(4, 3, 88, 128).transpose(3, 0, 1, 2).reshape(128, -1)
    cb = f(inp["ffn_conv_b"])
    sp[:, SP_CONVB:SP_CONVB + 4 * 88] = cb.reshape(4, 88, 128).transpose(2, 0, 1).reshape(128, -1)
    lb = f(inp["hg_lower_bounds"])
    sp[:, SP_LB:SP_LB + 64] = lb.reshape(4, 16, 128).transpose(2, 0, 1).reshape(128, -1)
    sp[:, SP_HGN:SP_HGN + 2] = f(inp["hg_norm_g"]).T
    sp[:, SP_SUBLN:SP_SUBLN + 4] = f(inp["da_subln_g"]).reshape(2, 2, 128).transpose(2, 0, 1).reshape(128, -1)
    sp[:, SP_LAM:SP_LAM + 8] = f(inp["da_lambda"]).transpose(2, 0, 1).reshape(128, -1)
    out["spd"] = sp
    cm = np.zeros((128, 4, 128), np.float32)
    cm[:, 0, :] = np.eye(128)
    for m in range(16):
        cm[m + 16, 1, m] = -1.0
        cm[m, 1, m + 16] = 1.0
    s_idx = np.arange(64)[:, None]; t_idx = np.arange(64)[None, :]
    cm[:64, 2, :64] = (s_idx <= t_idx)
    cm[:64, 3, :64] = (s_idx >= t_idx)
    out["cmat"] = cm.astype(BF)
    return out


def prep_core(x_core, coupled, T, shared):
    m = dict(shared)
    m["x_in"] = np.ascontiguousarray(x_core, dtype=np.float32)
    sp = shared["spd"].copy()
    sp[:, SP_FLAG] = 1.0 if coupled else 0.0
    m["spd"] = sp
    MT = T // 2
    pos = np.arange(T) if coupled else (np.arange(T) % MT)
    half = 16
    inv = 1.0 / (ROPE_THETA ** (np.arange(0, 32, 2, dtype=np.float32) / 32.0))
    ang = pos.astype(np.float32)[None, :] * inv.astype(np.float32)[:, None]
    cs = np.zeros((128, 2, T), np.float32)
    cs[:, 0] = 1.0
    cs[:16, 0] = np.cos(ang); cs[16:32, 0] = np.cos(ang)
    cs[:16, 1] = np.sin(ang); cs[16:32, 1] = np.sin(ang)
    m["cs"] = cs
    nk = T // 128
    mb = np.zeros((128, 2, nk), np.float32)
    if not coupled:
        mb[:, 0, nk // 2:] = -30000.0
        mb[:, 1, :nk // 2] = -30000.0
    m["maskb"] = mb
    return m


_CACHE = {}


def get_prog(T, layers=(0, 1, 2, 3), parts=("mix", "ffn"), **opts):
    key = (T, tuple(layers), tuple(parts), tuple(sorted(opts.items())))
    if key not in _CACHE:
        p = Prog(T, layers, parts)
        for k, v in opts.items():
            setattr(p, k, v)
        _CACHE[key] = p.build()
    return _CACHE[key]


def kernel(**inputs):
    T = 4096
    xp = np.asarray(inputs["x_prompt"], dtype=np.float32)
    xs = np.asarray(inputs["x_sample"], dtype=np.float32)
    shared = prep_shared(inputs, T)
    cores = [
        (xp[0:2].reshape(T, D), False),
        (xp[2:4].reshape(T, D), False),
        (xs[0], True),
        (xs[1], True),
    ]
    maps = [prep_core(x, c, T, shared) for (x, c) in cores]
    zero = {k: np.zeros_like(v) for k, v in maps[0].items()}
    active = [0, 1, 4, 5]
    in_maps = [zero] * 8
    in_maps = list(in_maps)
    for a, m in zip(active, maps):
        in_maps[a] = m
    nc = get_prog(T)
    res = run_bass_kernel_spmd(nc, in_maps, core_ids=list(range(8)))
    ys = [np.asarray(res.results[i]["y"], dtype=np.float32) for i in active]
    y_prompt = np.concatenate([ys[0].reshape(2, 2048, D), ys[1].reshape(2, 2048, D)], 0)
    y_sample = np.stack([ys[2], ys[3]], 0)
    return (y_prompt, y_sample)
```
